# Optimizing a Trainium2 kernel written in Bass

```python
import jax, jax.numpy as jnp
from jax import lax
import numpy as np

D_MODEL = 2048
BATCH = 16
SEQ = 256
DEPTH = 2
DEC_BATCH = 8
DEC_SEQ = 4096
PAST_LEN = 512

GRID_W = 64
D_CONV = D_MODEL // 2
CONV_W = 3
GLA_HEADS = 4
D_K = D_MODEL // 2
D_V = D_MODEL
DK_HEAD = D_K // GLA_HEADS
DV_HEAD = D_V // GLA_HEADS
GK_RANK = 16
GATE_NORM = 16.0
LOG_DECAY_MIN = -1.0
GLA_CHUNK = 64
N_KEYS = 128
N_EXPERTS = N_KEYS * N_KEYS
PEER_HEADS = 8
PEER_DK = 256
PEER_TOPK = 16
PEER_BLOCK = 128
N_ADA = 6
D_IN = 3 * D_CONV + 2 * D_K + 2 * D_V + 2 * GK_RANK + 2 * D_MODEL
ALPHA = (2.0 * DEPTH) ** 0.25
BETA = (8.0 * DEPTH) ** -0.25
LN_EPS = 1e-5

kernel_name = 'hybrid_conv_gla_peer_diffusion_step'


def layer_norm(x, gain=None, bias=None):
    xf = x.astype(jnp.float32)
    mu = jnp.mean(xf, axis=-1, keepdims=True)
    var = jnp.mean(jnp.square(xf - mu), axis=-1, keepdims=True)
    y = (xf - mu) * lax.rsqrt(var + LN_EPS)
    if gain is not None:
        y = y * gain.astype(jnp.float32) + bias.astype(jnp.float32)
    return y.astype(x.dtype)


def conv3(u, w, axis):
    n = u.shape[axis]
    pad = [(1, 1) if a == axis else (0, 0) for a in range(u.ndim)]
    up = jnp.pad(u, pad)
    return (lax.slice_in_dim(up, 0, n, axis=axis) * w[0]
            + lax.slice_in_dim(up, 1, n + 1, axis=axis) * w[1]
            + lax.slice_in_dim(up, 2, n + 2, axis=axis) * w[2])


def conv_latent(u, w):
    B, T, C = u.shape
    rows = T // GRID_W
    g = u.reshape(B, rows, GRID_W, C)
    half = C // 2
    gh = conv3(g[..., :half], w[:, :half], axis=2)
    gv = conv3(g[..., half:], w[:, half:], axis=1)
    return jnp.concatenate([gh, gv], axis=-1).reshape(B, T, C)


def to_heads(t):
    B, T, W = t.shape
    return t.reshape(B, T, GLA_HEADS, W // GLA_HEADS).transpose(0, 2, 1, 3)


def gla_chunked(q, k, v, g, s0):
    out_dtype = v.dtype
    B, H, T, dk = q.shape
    dv = v.shape[-1]
    n = T // GLA_CHUNK
    f32 = jnp.float32
    q = q.astype(f32).reshape(B, H, n, GLA_CHUNK, dk)
    k = k.astype(f32).reshape(B, H, n, GLA_CHUNK, dk)
    v = v.astype(f32).reshape(B, H, n, GLA_CHUNK, dv)
    b = jnp.cumsum(g.astype(f32).reshape(B, H, n, GLA_CHUNK, dk), axis=3)
    b_last = b[:, :, :, -1:, :]
    qe = q * jnp.exp(b)
    ke = k * jnp.exp(-b)
    kd = k * jnp.exp(b_last - b)
    tril = jnp.tril(jnp.ones((GLA_CHUNK, GLA_CHUNK), dtype=bool))
    att = jnp.where(tril, jnp.einsum('bhncd,bhnsd->bhncs', qe, ke), 0.0)
    o_intra = jnp.einsum('bhncs,bhnse->bhnce', att, v)
    decay = jnp.exp(b_last[:, :, :, 0, :])

    def step(S, xs):
        qe_c, kd_c, v_c, dl = xs
        o_c = jnp.einsum('bhcd,bhde->bhce', qe_c, S)
        S = S * dl[..., :, None] + jnp.einsum('bhcd,bhce->bhde', kd_c, v_c)
        return S, o_c

    xs = (jnp.moveaxis(qe, 2, 0), jnp.moveaxis(kd, 2, 0), jnp.moveaxis(v, 2, 0), jnp.moveaxis(decay, 2, 0))
    S, o_inter = lax.scan(step, s0.astype(f32), xs)
    o = (o_intra + jnp.moveaxis(o_inter, 0, 2)).reshape(B, H, T, dv)
    return o.astype(out_dtype), S


def parallel_mixer(h, latent, s_f0, s_b0, w_in, w_conv, w_a, w_gk_up, b_gk, w_gla_norm, w_b, w_o):
    B, T, _ = h.shape
    z = h @ w_in
    sizes = (D_CONV, D_CONV, D_CONV, D_K, D_K, D_V, D_V, GK_RANK, GK_RANK, D_MODEL, D_MODEL)
    cb, cc, cx, q, k, v, r, lf, lb, ga, gb = jnp.split(z, np.cumsum(sizes)[:-1].tolist(), axis=-1)
    u = cc * cx
    u = conv_latent(u, w_conv) if latent else conv3(u, w_conv, axis=1)
    y_a = (cb * u) @ w_a
    g_f = jnp.maximum(jax.nn.log_sigmoid((lf @ w_gk_up[0] + b_gk[0]).astype(jnp.float32)) / GATE_NORM, LOG_DECAY_MIN)
    g_b = jnp.maximum(jax.nn.log_sigmoid((lb @ w_gk_up[1] + b_gk[1]).astype(jnp.float32)) / GATE_NORM, LOG_DECAY_MIN)
    qh = to_heads(q * (DK_HEAD ** -0.5))
    kh = to_heads(k)
    vh = to_heads(v)
    o_f, s_f = gla_chunked(qh, kh, vh, to_heads(g_f), s_f0)
    flip = lambda t: jnp.flip(t, axis=2)
    o_b, s_b = gla_chunked(flip(qh), flip(kh), flip(vh), flip(to_heads(g_b)), s_b0)
    o = (o_f + flip(o_b)).astype(jnp.float32)
    o = o * lax.rsqrt(jnp.mean(jnp.square(o), axis=-1, keepdims=True) + LN_EPS)
    o = o * w_gla_norm.reshape(GLA_HEADS, 1, DV_HEAD).astype(jnp.float32)
    o = o.astype(h.dtype).transpose(0, 2, 1, 3).reshape(B, T, D_V) * jax.nn.silu(r)
    y_b = o @ w_b
    y = jax.nn.sigmoid(ga) * y_a + jax.nn.sigmoid(gb) * y_b
    return y @ w_o, s_f, s_b


def peer_ffn(h, w_pq, sub_keys, w_u, w_v):
    B, T, D = h.shape
    xb = h.reshape(-1, PEER_BLOCK, D)

    def block(x):
        qq = (x @ w_pq).reshape(PEER_BLOCK, PEER_HEADS, 2, PEER_DK // 2)
        s1 = jnp.einsum('phd,nd->phn', qq[:, :, 0], sub_keys[0])
        s2 = jnp.einsum('phd,nd->phn', qq[:, :, 1], sub_keys[1])
        v1, i1 = lax.top_k(s1, PEER_TOPK)
        v2, i2 = lax.top_k(s2, PEER_TOPK)
        cand = (v1[..., :, None] + v2[..., None, :]).reshape(PEER_BLOCK, PEER_HEADS, PEER_TOPK * PEER_TOPK)
        sv, si = lax.top_k(cand, PEER_TOPK)
        e1 = jnp.take_along_axis(i1, si // PEER_TOPK, axis=-1)
        e2 = jnp.take_along_axis(i2, si % PEER_TOPK, axis=-1)
        idx = e1 * N_KEYS + e2
        gate = jax.nn.softmax(sv.astype(jnp.float32), axis=-1).astype(x.dtype)
        act = jax.nn.gelu(jnp.einsum('pd,phkd->phk', x, w_u[idx]))
        return jnp.einsum('phk,phkd->pd', gate * act, w_v[idx])

    return lax.map(block, xb).reshape(B, T, D)


def trunk_layer(x, cond, latent, s_f0, s_b0, w_in, w_conv, w_a, w_gk_up, b_gk, w_gla_norm, w_b, w_o,
                w_ada, b_ada, ln_g, ln_b, w_pq, peer_keys, peer_u, peer_v):
    mod = jax.nn.silu(cond) @ w_ada + b_ada
    sh1, sc1, g1, sh2, sc2, g2 = jnp.split(mod[:, None, :], N_ADA, axis=-1)
    h = layer_norm(x) * (1.0 + sc1) + sh1
    m, s_f, s_b = parallel_mixer(h, latent, s_f0, s_b0, w_in, w_conv, w_a, w_gk_up, b_gk, w_gla_norm, w_b, w_o)
    x = layer_norm(ALPHA * x + g1 * m, ln_g[0], ln_b[0])
    h = layer_norm(x) * (1.0 + sc2) + sh2
    f = peer_ffn(h, w_pq, peer_keys, peer_u, peer_v)
    x = layer_norm(ALPHA * x + g2 * f, ln_g[1], ln_b[1])
    return x, s_f, s_b


def setup_inputs(seed: int = 0) -> dict:
    key = jax.random.key(seed)
    ks = jax.random.split(key, 24)
    f32 = jnp.float32

    def nrm(k, shape, s):
        return jax.random.normal(k, shape, f32) * s

    return {
        'x_prompt': nrm(ks[0], (BATCH, SEQ, D_MODEL), 1.0),
        'x_sample': nrm(ks[1], (DEC_BATCH, DEC_SEQ, D_MODEL), 1.0),
        'state_gla': nrm(ks[2], (DEC_BATCH, DEPTH, 2, GLA_HEADS, DK_HEAD, DV_HEAD), 0.5),
        'c': nrm(ks[3], (DEC_BATCH, D_MODEL), 1.0),
        'c_ctx': nrm(ks[4], (D_MODEL,), 1.0),
        'w_in': nrm(ks[5], (DEPTH, D_MODEL, D_IN), D_MODEL ** -0.5),
        'w_conv': nrm(ks[6], (DEPTH, CONV_W, D_CONV), CONV_W ** -0.5),
        'w_a': nrm(ks[7], (DEPTH, D_CONV, D_MODEL), D_CONV ** -0.5),
        'w_gk_up': nrm(ks[8], (DEPTH, 2, GK_RANK, D_K), GK_RANK ** -0.5),
        'b_gk': nrm(ks[9], (DEPTH, 2, D_K), 0.1),
        'w_gla_norm': 1.0 + nrm(ks[10], (DEPTH, D_V), 0.02),
        'w_b': nrm(ks[11], (DEPTH, D_V, D_MODEL), D_V ** -0.5),
        'w_o': nrm(ks[12], (DEPTH, D_MODEL, D_MODEL), BETA * D_MODEL ** -0.5),
        'w_ada': nrm(ks[13], (DEPTH, D_MODEL, N_ADA * D_MODEL), 0.5 * D_MODEL ** -0.5),
        'b_ada': nrm(ks[14], (DEPTH, N_ADA * D_MODEL), 0.02),
        'ln_g': 1.0 + nrm(ks[15], (DEPTH, 2, D_MODEL), 0.02),
        'ln_b': nrm(ks[16], (DEPTH, 2, D_MODEL), 0.02),
        'w_pq': nrm(ks[17], (DEPTH, D_MODEL, PEER_HEADS * PEER_DK), D_MODEL ** -0.5),
        'peer_keys': nrm(ks[18], (DEPTH, 2, N_KEYS, PEER_DK // 2), (PEER_DK // 2) ** -0.5),
        'peer_u': nrm(ks[19], (DEPTH, N_EXPERTS, D_MODEL), D_MODEL ** -0.5),
        'peer_v': nrm(ks[20], (DEPTH, N_EXPERTS, D_MODEL), BETA * 0.3),
    }


def reference(x_prompt, x_sample, state_gla, c, c_ctx, w_in, w_conv, w_a, w_gk_up, b_gk, w_gla_norm, w_b, w_o,
              w_ada, b_ada, ln_g, ln_b, w_pq, peer_keys, peer_u, peer_v):
    xp = x_prompt
    xs = x_sample
    zero_state = jnp.zeros((x_prompt.shape[0], GLA_HEADS, DK_HEAD, DV_HEAD), jnp.float32)
    cond_ctx = c_ctx[None, :]
    ctx_states = []
    for l in range(DEPTH):
        lw = (w_in[l], w_conv[l], w_a[l], w_gk_up[l], b_gk[l], w_gla_norm[l], w_b[l], w_o[l],
              w_ada[l], b_ada[l], ln_g[l], ln_b[l], w_pq[l], peer_keys[l], peer_u[l], peer_v[l])
        xp, s_f, s_b = trunk_layer(xp, cond_ctx, False, zero_state, zero_state, *lw)
        ctx_states.append(jnp.stack([s_f, s_b], axis=1))
        xs, _, _ = trunk_layer(xs, c, True, state_gla[:, l, 0], state_gla[:, l, 1], *lw)
    new_state_gla = jnp.stack(ctx_states, axis=1).astype(x_prompt.dtype)
    return (xp, xs, new_state_gla)
```

```python
import contextlib
import os
import numpy as np
import concourse.bass as bass
import concourse.mybir as mybir
from concourse.bass_utils import run_bass_kernel_spmd

F32 = mybir.dt.float32
BF16 = mybir.dt.bfloat16
AF = mybir.ActivationFunctionType
ALU = mybir.AluOpType
AX = mybir.AxisListType

D = 2048
DEPTH = 2
NTOK = 4608
TS = 4096
NT = NTOK // 128
DIN = 13344
ALPHA = (2.0 * DEPTH) ** 0.25
EPS = 1e-5
NSD = 8
GELU_C = 1.5957691216057308


class Sched:
    def __init__(self, nc, es):
        self.nc = nc
        self.eng = {'pe': nc.tensor, 'act': nc.scalar, 'dve': nc.vector, 'pool': nc.gpsimd, 'sp': nc.sync}
        self.sem = {e: es.enter_context(nc.semaphore('s_' + e)) for e in ('pe', 'act', 'dve', 'pool')}
        self.cnt = {e: 0 for e in self.sem}
        self.waited = {e: {} for e in self.eng}
        self.dsem = {q: [es.enter_context(nc.semaphore('d_%s%d' % (q, i))) for i in range(NSD)]
                     for q in ('sp', 'pool', 'act')}
        self.dcnt = {q: 0 for q in self.dsem}
        self.lastw = {}
        self.readers = {}

    def _wait(self, e, ev):
        key, val, semh = ev
        if e == 'pe' and key == ('c', 'pe'):
            return
        if self.waited[e].get(key, 0) < val:
            self.eng[e].wait_ge(semh, val)
            self.waited[e][key] = val

    def _deps(self, e, reads, writes):
        for k in reads:
            w = self.lastw.get(k)
            if w is not None:
                self._wait(e, w)
        for k in writes:
            w = self.lastw.get(k)
            if w is not None:
                self._wait(e, w)
            for r in self.readers.get(k, {}).values():
                self._wait(e, r)

    def _reg(self, ev, reads, writes):
        for k in reads:
            self.readers.setdefault(k, {})[ev[0]] = ev
        for k in writes:
            self.lastw[k] = ev
            self.readers[k] = {}

    def op(self, e, fn, reads=(), writes=()):
        self._deps(e, reads, writes)
        ins = fn(self.eng[e])
        self.cnt[e] += 1
        ins.then_inc(self.sem[e], 1)
        self._reg((('c', e), self.cnt[e], self.sem[e]), reads, writes)

    def dma(self, q, out, in_, reads=(), writes=(), slow=False):
        i = self.dcnt[q]
        self.dcnt[q] += 1
        slot = i % NSD
        val = 16 * (i // NSD + 1)
        semh = self.dsem[q][slot]
        key = ('d', q, slot)
        if val > 16:
            self._wait(q, (key, val - 16, semh))
        self._deps(q, reads, writes)
        if slow:
            ins = self.eng[q].dma_start(out=out, in_=in_, allow_slow_non_contiguous=True)
        else:
            ins = self.eng[q].dma_start(out=out, in_=in_)
        ins.then_inc(semh, 16)
        self._reg((key, val, semh), reads, writes)

    def barrier(self):
        evs = []
        for e in self.sem:
            if self.cnt[e] > 0:
                evs.append((('c', e), self.cnt[e], self.sem[e]))
        for q in self.dsem:
            n = self.dcnt[q]
            for slot in range(NSD):
                if n > slot:
                    k = (n - 1 - slot) // NSD + 1
                    evs.append((('d', q, slot), 16 * k, self.dsem[q][slot]))
        for e in self.eng:
            for ev in evs:
                key, val, semh = ev
                if self.waited[e].get(key, 0) < val:
                    self.eng[e].wait_ge(semh, val)
                    self.waited[e][key] = val


class Arena:
    def __init__(self, ar, nwords):
        self.ar = ar
        self.n = nwords
        self.off = 0

    def reset(self):
        self.off = 0

    def f32(self, shape):
        n = int(np.prod(shape[1:]))
        assert self.off + n <= self.n, ("arena overflow", self.off, n)
        v = self.ar[0:shape[0], self.off:self.off + n]
        self.off += n
        return _shape(v, shape)

    def bf16(self, shape):
        n = int(np.prod(shape[1:]))
        nw = (n + 1) // 2
        assert self.off + nw <= self.n, ("arena overflow", self.off, nw)
        v = self.ar[0:shape[0], self.off:self.off + nw].bitcast(BF16)[:, 0:n]
        self.off += nw
        return _shape(v, shape)


def _shape(v, shape):
    if len(shape) == 2:
        return v
    if len(shape) == 3:
        return v.rearrange("p (a b) -> p a b", a=shape[1])
    if len(shape) == 4:
        return v.rearrange("p (a b c) -> p a b c", a=shape[1], b=shape[2])
    raise ValueError(shape)


def build(stage=99):
    nc = bass.Bass("TRN2", target_bir_lowering=False)

    def din(name, shape, dt=F32):
        return nc.dram_tensor(name, list(shape), dt, kind="ExternalInput").ap()

    def dout(name, shape, dt=F32):
        return nc.dram_tensor(name, list(shape), dt, kind="ExternalOutput").ap()

    def dscr(name, shape, dt=F32):
        kind = "ExternalOutput" if os.environ.get("MK_DEBUG") else "Internal"
        return nc.dram_tensor(name, list(shape), dt, kind=kind).ap()

    xs = din("xs", [TS, D]); xp = din("xp", [512, D])
    st0 = din("st0", [DEPTH, 2, 4, 256, 512])
    cvec = din("cvec", [2, D])
    w_in = din("w_in", [DEPTH, D, DIN]); w_conv = din("w_conv", [DEPTH, 3, 1024])
    w_a = din("w_a", [DEPTH, 1024, D]); w_gk = din("w_gk_up", [DEPTH, 2, 16, 1024])
    b_gk = din("b_gk", [DEPTH, 2, 1024]); w_gn = din("w_gla_norm", [DEPTH, D])
    w_b = din("w_b", [DEPTH, D, D]); w_o = din("w_o", [DEPTH, D, D])
    w_ada = din("w_ada", [DEPTH, D, 6 * D]); b_ada = din("b_ada", [DEPTH, 6 * D])
    ln_g = din("ln_g", [DEPTH, 2, D]); ln_b = din("ln_b", [DEPTH, 2, D])
    w_pq = din("w_pq", [DEPTH, D, D]); pkeys = din("peer_keys", [DEPTH, 2, 128, 128])
    peer_u = din("peer_u", [DEPTH, 16384, D]); peer_v = din("peer_v", [DEPTH, 16384, D])
    c_ident = din("c_ident", [128, 128]); c_mf = din("c_mf", [64, 64]); c_mb = din("c_mb", [64, 64])
    c_rm = din("c_rm", [128, 512])

    ys = dout("ys", [TS, D]); yp = dout("yp", [512, D])
    ns = dout("ns", [2, DEPTH, 2, 4, 256, 512])

    ZT = dscr("ZT", [9248, NTOK]); Z = dscr("Z", [NTOK, 4096])
    VAT = dscr("VAT", [1024, NTOK], BF16)
    OF = dscr("OF", [NTOK, D]); OB = dscr("OB", [NTOK, D])
    X1 = dscr("X1", [NTOK, D]); X2 = dscr("X2", [NTOK, D])
    MOD = dscr("MOD", [2, 6 * D])
    WUT = dscr("WUT", [64, 128, 4096], BF16); WVB = dscr("WVB", [16384, D], BF16)

    es = contextlib.ExitStack()
    with es:
        S = Sched(nc, es)
        NW = 51200
        ARt = es.enter_context(nc.sbuf_tensor("arena", [128, NW], F32))
        AR = Arena(ARt, NW)
        ident = es.enter_context(nc.sbuf_tensor("ident", [128, 128], F32))
        mf = es.enter_context(nc.sbuf_tensor("mf", [64, 64], F32))
        mb = es.enter_context(nc.sbuf_tensor("mb", [64, 64], F32))
        rmk = es.enter_context(nc.sbuf_tensor("rmk", [128, 512], F32))
        modT = es.enter_context(nc.sbuf_tensor("modT", [128, 2, 4, 16], F32))
        small = es.enter_context(nc.sbuf_tensor("small", [128, 256], F32))
        PS = [es.enter_context(nc.psum_tensor("ps%d" % i, [128, 512], F32)) for i in range(8)]
        PK = ["ps%d" % i for i in range(8)]

        S.dma('sp', ident[:], c_ident[:, :], writes=['ident'])
        S.dma('sp', mf[:], c_mf[:, :], writes=['mf'])
        S.dma('sp', mb[:], c_mb[:, :], writes=['mb'])
        S.dma('sp', rmk[:], c_rm[:, :], writes=['rmk'])

        def xsrc(l, t0, n):
            if l == 0:
                return xs[t0:t0 + n, :] if t0 < TS else xp[t0 - TS:t0 - TS + n, :]
            return X2[t0:t0 + n, :]

        def xdst(l, t0, n):
            if l == DEPTH - 1:
                return ys[t0:t0 + n, :] if t0 < TS else yp[t0 - TS:t0 - TS + n, :]
            return X2[t0:t0 + n, :]

        rr = {'ev': 0}

        def evac(out, in_, reads, writes):
            rr['ev'] += 1
            if rr['ev'] % 2:
                S.op('act', lambda e: e.activation(out=out, in_=in_, func=AF.Copy), reads, writes)
            else:
                S.op('dve', lambda e: e.tensor_copy(out, in_), reads, writes)

        def ln_stats(xt, xk, sm, smk):
            for j in range(4):
                S.op('dve', lambda e, j=j: e.bn_stats(sm[:, j * 6:(j + 1) * 6], xt[:, j * 512:(j + 1) * 512]),
                     [xk], [smk])
            S.op('dve', lambda e: e.bn_aggr(sm[:, 24:26], sm[:, 0:24]), [smk], [smk])
            S.op('act', lambda e: e.activation(out=sm[:, 26:27], in_=sm[:, 25:26], func=AF.Sqrt, bias=EPS, scale=1.0),
                 [smk], [smk])
            S.op('dve', lambda e: e.reciprocal(sm[:, 27:28], sm[:, 26:27]), [smk], [smk])
            S.op('dve', lambda e: e.scalar_tensor_tensor(out=sm[:, 28:29], in0=sm[:, 24:25], scalar=-1.0,
                                                          in1=sm[:, 27:28], op0=ALU.mult, op1=ALU.mult),
                 [smk], [smk])
            return sm[:, 27:28], sm[:, 28:29]

        def ln_transpose(src_ap, xt, xk, xn, xnk, sm, smk, HT, hk, col0, cond, jsh, jsc, psb):
            S.dma('sp', xt, src_ap, writes=[xk])
            rstd, nb = ln_stats(xt, xk, sm, smk)
            S.op('act', lambda e: e.activation(out=xn, in_=xt, func=AF.Identity, bias=nb, scale=rstd),
                 [xk, smk], [xnk])
            for q4 in range(4):
                b = psb[q4 % len(psb)]
                for j in range(4):
                    kc = q4 * 4 + j
                    S.op('pe', lambda e, kc=kc, j=j, b=b: e.transpose(PS[b][:, j * 128:(j + 1) * 128],
                                                                      xn[:, kc * 128:(kc + 1) * 128], ident[:]),
                         [xnk, 'ident'], [PK[b]])
                for j in range(4):
                    kc = q4 * 4 + j
                    o_ = HT[:, kc, col0:col0 + 128]
                    i_ = PS[b][:, j * 128:(j + 1) * 128]
                    if cond is None:
                        evac(o_, i_, [PK[b]], [hk])
                    elif kc % 2:
                        S.op('dve', lambda e, o_=o_, i_=i_, kc=kc: e.tensor_scalar(
                            o_, i_, modT[:, cond, jsc, kc:kc + 1], modT[:, cond, jsh, kc:kc + 1], ALU.mult, ALU.add),
                             [PK[b], 'modT'], [hk])
                    else:
                        S.op('act', lambda e, o_=o_, i_=i_, kc=kc: e.activation(
                            out=o_, in_=i_, func=AF.Identity, bias=modT[:, cond, jsh, kc:kc + 1],
                            scale=modT[:, cond, jsc, kc:kc + 1]), [PK[b], 'modT'], [hk])

        def post_ln(l, which, t0, cond, acc_tiles, xin_ap, dst_ap, bufs, ek):
            xt, xn, GB_, LG, LB, sm = bufs
            S.dma('sp', xt, xin_ap, writes=['pl_x'] + ek)
            for cb in range(4):
                sl = slice(cb * 512, (cb + 1) * 512)
                S.op('dve', lambda e, cb=cb, sl=sl: e.tensor_tensor(xn[:, sl], PS[acc_tiles[cb]][:, :], GB_[:, sl],
                                                                     ALU.mult),
                     [PK[acc_tiles[cb]], 'GB'], ['pl_n%d' % cb] + ek)
                S.op('dve', lambda e, sl=sl: e.scalar_tensor_tensor(out=xn[:, sl], in0=xt[:, sl], scalar=ALPHA,
                                                                     in1=xn[:, sl], op0=ALU.mult, op1=ALU.add),
                     ['pl_x', 'pl_n%d' % cb], ['pl_n%d' % cb] + ek)
            allk = ['pl_n%d' % cb for cb in range(4)]
            S.op('pool', lambda e: e.tensor_copy(xt, xn), allk, ['pl_x'] + ek)
            rstd, nb = ln_stats(xt, 'pl_x', sm, 'small')
            S.op('act', lambda e: e.activation(out=xn, in_=xt, func=AF.Identity, bias=nb, scale=rstd),
                 ['pl_x', 'small'], allk + ek)
            S.op('dve', lambda e: e.tensor_tensor(xt, xn, LG, ALU.mult), allk + ['LG'], ['pl_x'] + ek)
            S.op('pool', lambda e: e.tensor_tensor(xn, xt, LB, ALU.add), ['pl_x', 'LB'], allk + ek)
            S.dma('sp', dst_ap, xn, reads=allk)

        for l in range(int(os.environ.get("MK_LAYERS", DEPTH))):
            AR.reset()
            CT = AR.f32([128, 2, 16]); BA = AR.f32([2, 6 * D]); MS = AR.f32([2, 6 * D])
            WF = [AR.f32([128, 16, 512]) for _ in range(2)]
            for c in range(2):
                S.dma('sp', CT[:, c, :], cvec[c].rearrange("(kc p) -> p kc", p=128), writes=['CT'], slow=True)
            S.dma('sp', BA, b_ada[l:l + 1, :].broadcast_to([2, 6 * D]), writes=['BA'])
            S.op('act', lambda e: e.activation(out=CT, in_=CT, func=AF.Silu), ['CT'], ['CT'])
            for cb in range(24):
                wf = WF[cb % 2]; wk = 'WF%d' % (cb % 2)
                S.dma('sp' if cb % 2 else 'act', wf,
                      w_ada[l, :, cb * 512:(cb + 1) * 512].rearrange("(kc p) n -> p kc n", p=128), writes=[wk])
                b = cb % 2
                for kc in range(16):
                    S.op('pe', lambda e, kc=kc, wf=wf, b=b: e.matmul(PS[b][0:2, :], CT[:, :, kc], wf[:, kc, :],
                                                                     start=(kc == 0), stop=(kc == 15)),
                         ['CT', wk], [PK[b]])
                S.op('dve', lambda e, cb=cb, b=b: e.tensor_tensor(MS[:, cb * 512:(cb + 1) * 512], PS[b][0:2, :],
                                                                  BA[:, cb * 512:(cb + 1) * 512], ALU.add),
                     [PK[b], 'BA'], ['MS'])
            S.dma('sp', MOD[:, :], MS, reads=['MS'])
            S.barrier()
            for c in range(2):
                for jj, j in enumerate((0, 1, 3, 4)):
                    S.dma('sp', modT[:, c, jj, :], MOD[c, j * D:(j + 1) * D].rearrange("(kc p) -> p kc", p=128),
                          writes=['modT'], slow=True)
            for c in range(2):
                for jj in (1, 3):
                    S.op('dve', lambda e, c=c, jj=jj: e.tensor_scalar_add(modT[:, c, jj, :], modT[:, c, jj, :], 1.0),
                         ['modT'], ['modT'])
            if stage <= 0:
                continue

            AR.reset()
            WB = [AR.bf16([128, 16, 512]) for _ in range(2)]
            HT = AR.bf16([128, 16, 512])
            XT = [AR.f32([128, D]) for _ in range(2)]
            XN = AR.f32([128, D])
            STG = [AR.f32([128, 512]) for _ in range(4)]
            fm_blocks = [(c0, 512, c0) for c0 in range(0, 5120, 512)] + [(9216, 32, 5120)] + \
                        [(9248 + i * 512, 512, 5152 + i * 512) for i in range(8)]
            tm_blocks = [(5120 + i * 512, 512, i * 512) for i in range(8)]
            wi = 0; si = 0; pi = 0
            for g in range(9):
                for t in range(4):
                    tile = g * 4 + t
                    cond = 0 if tile < 32 else 1
                    ln_transpose(xsrc(l, tile * 128, 128), XT[t % 2], 'XT%d' % (t % 2), XN, 'XN', small, 'small',
                                 HT, 'HT', t * 128, cond, 0, 1, [4, 5, 6, 7])
                for (c0, ncol, r0) in fm_blocks:
                    wb = WB[wi % 2]; wk = 'WB%d' % (wi % 2); wi += 1
                    S.dma('pool', wb[:, :, 0:ncol],
                          w_in[l, :, c0:c0 + ncol].rearrange("(kc p) n -> p kc n", p=128), writes=[wk])
                    for sub in range((ncol + 127) // 128):
                        m = min(128, ncol - sub * 128)
                        b = pi % 4; pi += 1
                        for kc in range(16):
                            S.op('pe', lambda e, kc=kc, wb=wb, b=b, m=m, sub=sub: e.matmul(
                                PS[b][0:m, :], wb[:, kc, sub * 128:sub * 128 + m], HT[:, kc, :],
                                start=(kc == 0), stop=(kc == 15)), [wk, 'HT'], [PK[b]])
                        stg = STG[si % 4]; sk = 'STG%d' % (si % 4); si += 1
                        evac(stg[0:m, :], PS[b][0:m, :], [PK[b]], [sk])
                        S.dma('sp', ZT[r0 + sub * 128:r0 + sub * 128 + m, g * 512:(g + 1) * 512], stg[0:m, :],
                              reads=[sk])
                for (c0, ncol, z0) in tm_blocks:
                    wb = WB[wi % 2]; wk = 'WB%d' % (wi % 2); wi += 1
                    S.dma('pool', wb, w_in[l, :, c0:c0 + 512].rearrange("(kc p) n -> p kc n", p=128), writes=[wk])
                    for t in range(4):
                        b = pi % 4; pi += 1
                        for kc in range(16):
                            S.op('pe', lambda e, kc=kc, wb=wb, b=b, t=t: e.matmul(
                                PS[b][:, :], HT[:, kc, t * 128:(t + 1) * 128], wb[:, kc, :],
                                start=(kc == 0), stop=(kc == 15)), [wk, 'HT'], [PK[b]])
                        stg = STG[si % 4]; sk = 'STG%d' % (si % 4); si += 1
                        evac(stg, PS[b][:, :], [PK[b]], [sk])
                        tok0 = (g * 4 + t) * 128
                        S.dma('sp', Z[tok0:tok0 + 128, z0:z0 + 512], stg, reads=[sk])
            S.barrier()
            if stage <= 1:
                continue

            AR.reset()
            WC = AR.f32([128, 3, 8])
            for j in range(3):
                S.dma('sp', WC[:, j, :], w_conv[l, j].rearrange("(b p) -> p b", p=128), writes=['WC'], slow=True)
            CC = AR.f32([128, 4096]); CX = AR.f32([128, 4096]); CB = AR.f32([128, 4096]); CO = AR.f32([128, 4096])
            VA = AR.bf16([128, 4096])
            for (t0, T, latent) in ((0, 4096, True), (4096, 512, False)):
                for blk in range(8):
                    r = blk * 128
                    S.dma('sp', CC[:, 0:T], ZT[1024 + r:1024 + r + 128, t0:t0 + T], writes=['CC'])
                    S.dma('act', CX[:, 0:T], ZT[2048 + r:2048 + r + 128, t0:t0 + T], writes=['CX'])
                    S.dma('sp', CB[:, 0:T], ZT[r:r + 128, t0:t0 + T], writes=['CB'])
                    S.op('pool', lambda e, T=T: e.tensor_tensor(CC[:, 0:T], CC[:, 0:T], CX[:, 0:T], ALU.mult),
                         ['CC', 'CX'], ['CC'])
                    S.op('dve', lambda e, T=T, blk=blk: e.tensor_scalar(CO[:, 0:T], CC[:, 0:T], WC[:, 1, blk:blk + 1], None,
                                                                         ALU.mult), ['CC', 'WC'], ['CO'])
                    if latent and blk < 4:
                        u3 = CC[:, 0:T].rearrange("p (r c) -> p r c", c=64)
                        o3 = CO[:, 0:T].rearrange("p (r c) -> p r c", c=64)
                        pairs = [(o3[:, :, 1:64], u3[:, :, 0:63], 0), (o3[:, :, 0:63], u3[:, :, 1:64], 2)]
                    elif latent:
                        pairs = [(CO[:, 64:T], CC[:, 0:T - 64], 0), (CO[:, 0:T - 64], CC[:, 64:T], 2)]
                    else:
                        u3 = CC[:, 0:T].rearrange("p (r c) -> p r c", c=256)
                        o3 = CO[:, 0:T].rearrange("p (r c) -> p r c", c=256)
                        pairs = [(o3[:, :, 1:256], u3[:, :, 0:255], 0), (o3[:, :, 0:255], u3[:, :, 1:256], 2)]
                    for (oo, uu, j) in pairs:
                        S.op('dve', lambda e, oo=oo, uu=uu, j=j, blk=blk: e.scalar_tensor_tensor(
                            out=oo, in0=uu, scalar=WC[:, j, blk:blk + 1], in1=oo, op0=ALU.mult, op1=ALU.add),
                             ['CC', 'CO', 'WC'], ['CO'])
                    S.op('pool', lambda e, T=T: e.tensor_tensor(VA[:, 0:T], CB[:, 0:T], CO[:, 0:T], ALU.mult),
                         ['CB', 'CO'], ['VA'])
                    S.dma('sp', VAT[r:r + 128, t0:t0 + T], VA[:, 0:T], reads=['VA'])
            S.barrier()
            if stage <= 2:
                continue

            for dr in range(2):
                AR.reset()
                OD = OF if dr == 0 else OB
                msk = mf if dr == 0 else mb
                mk = 'mf' if dr == 0 else 'mb'
                WGf = AR.f32([16, 1024]); WG = AR.bf16([16, 1024]); NBG = AR.f32([128, 8])
                S.dma('sp', WGf, w_gk[l, dr], writes=['WGf'])
                S.op('dve', lambda e: e.tensor_copy(WG, WGf), ['WGf'], ['WG'])
                S.dma('sp', NBG, b_gk[l, dr].rearrange("(j p) -> p j", p=128), writes=['NBG'], slow=True)
                S.op('dve', lambda e: e.tensor_scalar(NBG, NBG, -1.0, None, ALU.mult), ['NBG'], ['NBG'])
                LFf = AR.f32([16, 512]); LFb = AR.bf16([16, 512])
                Sst = [AR.f32([128, 2, 512]) for _ in range(4)]
                Sbf = [AR.bf16([128, 2, 512]) for _ in range(4)]
                QK = [AR.f32([128, 2, 2, 512]) for _ in range(2)]
                Gt = AR.f32([128, 512]); Bt = AR.f32([128, 512]); B2 = AR.f32([128, 512]); Et = AR.f32([128, 512])
                DEC = [AR.f32([128, 2, 8]) for _ in range(4)]
                QE = [AR.bf16([128, 2, 512]) for _ in range(4)]
                KE = [AR.bf16([128, 2, 512]) for _ in range(4)]
                KD32 = AR.f32([128, 512])
                KDT = [AR.bf16([64, 8, 256]) for _ in range(4)]
                Vt = [AR.bf16([64, 8, 512]) for _ in range(4)]
                AM = [AR.bf16([64, 64]) for _ in range(4)]
                OST = [AR.f32([64, 512]) for _ in range(4)]
                seqs = [(0, 4096, 512, None), (4096, 256, 256, 0), (4352, 256, 256, 1)]
                for (s0, T, BT, pbi) in seqs:
                    nch = BT // 64
                    for h in range(4):
                        if pbi is None:
                            S.dma('sp', Sst[h], st0[l, dr, h].rearrange("(kc p) e -> p kc e", p=128),
                                  writes=['S%d' % h])
                        else:
                            S.op('pool', lambda e, h=h: e.memset(Sst[h], 0.0), [], ['S%d' % h])
                        S.op('pool', lambda e, h=h: e.tensor_copy(Sbf[h], Sst[h]), ['S%d' % h], ['Sb%d' % h])
                    nblk = T // BT
                    order = range(nblk) if dr == 0 else range(nblk - 1, -1, -1)
                    for blk in order:
                        t0 = s0 + blk * BT
                        S.dma('sp', LFf[:, 0:BT], ZT[5120 + dr * 16:5136 + dr * 16, t0:t0 + BT], writes=['LFf'])
                        S.op('dve', lambda e, BT=BT: e.tensor_copy(LFb[:, 0:BT], LFf[:, 0:BT]), ['LFf'], ['LFb'])
                        for h in range(4):
                            qk = QK[h % 2]; qkk = 'QK%d' % (h % 2)
                            S.dma('sp', qk[:, 0, :, 0:BT],
                                  ZT[3072 + h * 256:3072 + (h + 1) * 256, t0:t0 + BT].rearrange("(kc p) t -> p kc t",
                                                                                                 p=128),
                                  writes=[qkk])
                            S.dma('act', qk[:, 1, :, 0:BT],
                                  ZT[4096 + h * 256:4096 + (h + 1) * 256, t0:t0 + BT].rearrange("(kc p) t -> p kc t",
                                                                                                 p=128),
                                  writes=[qkk])
                            S.dma('pool', Vt[h][:, 0:nch, :],
                                  Z[t0:t0 + BT, h * 512:(h + 1) * 512].rearrange("(n c) e -> c n e", c=64),
                                  writes=['V%d' % h])
                            for kc in range(2):
                                j = h * 2 + kc
                                b = 4 + (j % 2)
                                S.op('pe', lambda e, j=j, b=b, BT=BT: e.matmul(
                                    PS[b][:, 0:BT], WG[:, j * 128:(j + 1) * 128], LFb[:, 0:BT], start=True, stop=True),
                                     ['WG', 'LFb'], [PK[b]])
                                S.op('act', lambda e, j=j, b=b, BT=BT: e.activation(
                                    out=Et[:, 0:BT], in_=PS[b][:, 0:BT], func=AF.Exp, bias=NBG[:, j:j + 1], scale=-1.0),
                                     [PK[b], 'NBG'], ['Et'])
                                S.op('act', lambda e, BT=BT: e.activation(out=Et[:, 0:BT], in_=Et[:, 0:BT], func=AF.Ln,
                                                                          bias=1.0, scale=1.0), ['Et'], ['Et'])
                                S.op('dve', lambda e, BT=BT: e.tensor_scalar(Gt[:, 0:BT], Et[:, 0:BT], -1.0 / 16.0, -1.0,
                                                                             ALU.mult, ALU.max), ['Et'], ['Gt'])
                                S.op('dve', lambda e, BT=BT: e.tensor_tensor_scan(
                                    Bt[:, 0:BT], rmk[:, 0:BT], Gt[:, 0:BT], 0.0, ALU.mult, ALU.add),
                                     ['Gt', 'rmk'], ['Bt'])
                                B3 = Bt[:, 0:BT].rearrange("p (n c) -> p n c", c=64)
                                if dr == 1:
                                    S.op('dve', lambda e, BT=BT: e.tensor_tensor(B2[:, 0:BT], Gt[:, 0:BT], Bt[:, 0:BT],
                                                                                 ALU.subtract), ['Gt', 'Bt'], ['B2'])
                                    B23 = B2[:, 0:BT].rearrange("p (n c) -> p n c", c=64)
                                    S.op('dve', lambda e, B23=B23, B3=B3, nch=nch: e.tensor_tensor(
                                        B23, B23, B3[:, :, 63:64].broadcast_to([128, nch, 64]), ALU.add),
                                         ['B2', 'Bt'], ['B2'])
                                    Bu = B2; Buk = 'B2'; blast = B23[:, :, 0]
                                else:
                                    Bu = Bt; Buk = 'Bt'; blast = B3[:, :, 63]
                                dec = DEC[h][:, kc, 0:nch]
                                S.op('act', lambda e, dec=dec, blast=blast: e.activation(out=dec, in_=blast, func=AF.Exp),
                                     [Buk], ['DEC%d' % h])
                                S.op('act', lambda e, Bu=Bu, BT=BT: e.activation(out=Et[:, 0:BT], in_=Bu[:, 0:BT],
                                                                                  func=AF.Exp), [Buk], ['Et'])
                                S.op('dve', lambda e, h=h, kc=kc, qk=qk, BT=BT: e.scalar_tensor_tensor(
                                    out=QE[h][:, kc, 0:BT], in0=qk[:, 0, kc, 0:BT], scalar=1.0 / 16.0, in1=Et[:, 0:BT],
                                    op0=ALU.mult, op1=ALU.mult), [qkk, 'Et'], ['QE%d' % h])
                                S.op('act', lambda e, Bu=Bu, BT=BT: e.activation(out=Et[:, 0:BT], in_=Bu[:, 0:BT],
                                                                                  func=AF.Exp, scale=-1.0), [Buk], ['Et'])
                                S.op('dve', lambda e, h=h, kc=kc, qk=qk, BT=BT: e.tensor_tensor(
                                    qk[:, 1, kc, 0:BT], qk[:, 1, kc, 0:BT], Et[:, 0:BT], ALU.mult), [qkk, 'Et'], [qkk])
                                S.op('pool', lambda e, h=h, kc=kc, qk=qk, BT=BT: e.tensor_copy(
                                    KE[h][:, kc, 0:BT], qk[:, 1, kc, 0:BT]), [qkk], ['KE%d' % h])
                                K3 = qk[:, 1, kc, 0:BT].rearrange("p (n c) -> p n c", c=64)
                                D3 = KD32[:, 0:BT].rearrange("p (n c) -> p n c", c=64)
                                S.op('dve', lambda e, K3=K3, D3=D3, dec=dec, nch=nch: e.tensor_tensor(
                                    D3, K3, dec.unsqueeze(2).broadcast_to([128, nch, 64]), ALU.mult),
                                     [qkk, 'DEC%d' % h], ['KD32'])
                                for cq in range(0, nch, 4):
                                    bb = 6 + (cq // 4 + kc) % 2
                                    for c4 in range(4):
                                        ch = cq + c4
                                        S.op('pe', lambda e, ch=ch, c4=c4, bb=bb: e.transpose(
                                            PS[bb][0:64, c4 * 128:(c4 + 1) * 128], KD32[:, ch * 64:(ch + 1) * 64],
                                            ident[:]), ['KD32', 'ident'], [PK[bb]])
                                    evac(KDT[h][:, cq:cq + 4, kc * 128:(kc + 1) * 128],
                                         PS[bb][0:64, :].rearrange("p (n d) -> p n d", d=128), [PK[bb]], ['KDT%d' % h])
                        corder = range(nch) if dr == 0 else range(nch - 1, -1, -1)
                        for ch in corder:
                            cs = slice(ch * 64, (ch + 1) * 64)
                            for h in range(4):
                                S.op('pe', lambda e, h=h, cs=cs: e.matmul(PS[0][0:64, h * 64:(h + 1) * 64],
                                                                          KE[h][:, 0, cs], QE[h][:, 0, cs],
                                                                          start=True, stop=False),
                                     ['KE%d' % h, 'QE%d' % h], ['psA%d' % h])
                                S.op('pe', lambda e, h=h, cs=cs: e.matmul(PS[0][0:64, h * 64:(h + 1) * 64],
                                                                          KE[h][:, 1, cs], QE[h][:, 1, cs],
                                                                          start=False, stop=True),
                                     ['KE%d' % h, 'QE%d' % h], ['psA%d' % h])
                                S.op('dve', lambda e, h=h: e.tensor_tensor(AM[h], PS[0][0:64, h * 64:(h + 1) * 64], msk[:],
                                                                           ALU.mult), ['psA%d' % h, mk], ['AM%d' % h])
                                bo = 1 + (h % 2)
                                S.op('pe', lambda e, h=h, cs=cs, bo=bo: e.matmul(PS[bo][0:64, :], QE[h][:, 0, cs],
                                                                                 Sbf[h][:, 0, :], start=True, stop=False),
                                     ['QE%d' % h, 'Sb%d' % h], [PK[bo]])
                                S.op('pe', lambda e, h=h, cs=cs, bo=bo: e.matmul(PS[bo][0:64, :], QE[h][:, 1, cs],
                                                                                 Sbf[h][:, 1, :], start=False, stop=False),
                                     ['QE%d' % h, 'Sb%d' % h], [PK[bo]])
                                S.op('pe', lambda e, h=h, ch=ch, bo=bo: e.matmul(PS[bo][0:64, :], AM[h][:, :],
                                                                                 Vt[h][:, ch, :], start=False, stop=True),
                                     ['AM%d' % h, 'V%d' % h], [PK[bo]])
                                S.op('act', lambda e, h=h, bo=bo: e.activation(out=OST[h], in_=PS[bo][0:64, :],
                                                                               func=AF.Copy), [PK[bo]], ['OST%d' % h])
                                tk = t0 + ch * 64
                                S.dma('sp', OD[tk:tk + 64, h * 512:(h + 1) * 512], OST[h], reads=['OST%d' % h])
                                for kc in range(2):
                                    bs = 3 + ((h * 2 + kc) % 4) if False else (3 + (h * 2 + kc) % 2)
                                    S.op('pe', lambda e, h=h, ch=ch, kc=kc, bs=bs: e.matmul(
                                        PS[bs][:, :], KDT[h][:, ch, kc * 128:(kc + 1) * 128], Vt[h][:, ch, :],
                                        start=True, stop=True), ['KDT%d' % h, 'V%d' % h], [PK[bs]])
                                    S.op('dve', lambda e, h=h, ch=ch, kc=kc, bs=bs: e.scalar_tensor_tensor(
                                        out=Sst[h][:, kc, :], in0=Sst[h][:, kc, :], scalar=DEC[h][:, kc, ch:ch + 1],
                                        in1=PS[bs][:, :], op0=ALU.mult, op1=ALU.add),
                                         [PK[bs], 'DEC%d' % h, 'S%d' % h], ['S%d' % h])
                                S.op('pool', lambda e, h=h: e.tensor_copy(Sbf[h], Sst[h]), ['S%d' % h], ['Sb%d' % h])
                    if pbi is not None:
                        for h in range(4):
                            S.dma('sp', ns[pbi, l, dr, h].rearrange("(kc p) e -> p kc e", p=128), Sst[h],
                                  reads=['S%d' % h])
                S.barrier()
            if stage <= 3:
                continue

            AR.reset()
            WB = [AR.bf16([128, 16, 512]) for _ in range(2)]
            WA = [AR.bf16([128, 8, 512]) for _ in range(2)]
            HT = AR.bf16([128, 16, 512]); YT = AR.bf16([128, 16, 512]); VAs = AR.bf16([128, 8, 512])
            XT = AR.f32([128, D]); XN = AR.f32([128, D]); XO = AR.f32([128, D])
            WNB = AR.f32([128, D]); G1B = AR.f32([128, D]); LG = AR.f32([128, D]); LB = AR.f32([128, D])
            GAt = [AR.f32([128, 512]) for _ in range(2)]; GBt = [AR.f32([128, 512]) for _ in range(2)]
            S.dma('sp', WNB, w_gn[l:l + 1, :].broadcast_to([128, D]), writes=['WNB'])
            S.dma('sp', LG, ln_g[l, 0:1, :].broadcast_to([128, D]), writes=['LG'])
            S.dma('sp', LB, ln_b[l, 0:1, :].broadcast_to([128, D]), writes=['LB'])
            wi = 0; gi = 0
            for g in range(9):
                cond = 0 if g < 8 else 1
                if g == 0 or g == 8:
                    S.dma('sp', G1B, MOD[cond:cond + 1, 2 * D:3 * D].broadcast_to([128, D]), writes=['GB'])
                for t in range(4):
                    tok0 = (g * 4 + t) * 128
                    S.dma('sp', XT, OF[tok0:tok0 + 128, :], writes=['XT'])
                    S.dma('act', XN, OB[tok0:tok0 + 128, :], writes=['XN'])
                    S.dma('sp', XO, Z[tok0:tok0 + 128, 2048:4096], writes=['XO'])
                    S.op('pool', lambda e: e.tensor_tensor(XT, XT, XN, ALU.add), ['XT', 'XN'], ['XT'])
                    for h in range(4):
                        S.op('act', lambda e, h=h: e.activation(out=XN[:, h * 512:(h + 1) * 512],
                                                                in_=XT[:, h * 512:(h + 1) * 512], func=AF.Square,
                                                                accum_out=small[:, 32 + h:33 + h]),
                             ['XT'], ['XN', 'small'])
                    S.op('act', lambda e: e.activation(out=small[:, 36:40], in_=small[:, 32:36], func=AF.Sqrt, bias=EPS,
                                                       scale=1.0 / 512.0), ['small'], ['small'])
                    S.op('dve', lambda e: e.reciprocal(small[:, 40:44], small[:, 36:40]), ['small'], ['small'])
                    S.op('act', lambda e: e.activation(out=XO, in_=XO, func=AF.Silu), ['XO'], ['XO'])
                    for h in range(4):
                        hs = slice(h * 512, (h + 1) * 512)
                        S.op('dve', lambda e, h=h, hs=hs: e.scalar_tensor_tensor(
                            out=XN[:, hs], in0=XT[:, hs], scalar=small[:, 40 + h:41 + h], in1=WNB[:, hs],
                            op0=ALU.mult, op1=ALU.mult), ['XT', 'small', 'WNB'], ['XN'])
                    S.op('pool', lambda e: e.tensor_tensor(XN, XN, XO, ALU.mult), ['XN', 'XO'], ['XN'])
                    for q4 in range(4):
                        b = 4 + q4
                        for j in range(4):
                            kc = q4 * 4 + j
                            S.op('pe', lambda e, kc=kc, j=j, b=b: e.transpose(PS[b][:, j * 128:(j + 1) * 128],
                                                                              XN[:, kc * 128:(kc + 1) * 128], ident[:]),
                                 ['XN', 'ident'], [PK[b]])
                        evac(HT[:, q4 * 4:q4 * 4 + 4, t * 128:(t + 1) * 128],
                             PS[b][:, :].rearrange("p (n d) -> p n d", d=128), [PK[b]], ['HT'])
                S.dma('sp', VAs, VAT[:, g * 512:(g + 1) * 512].rearrange("(kc p) t -> p kc t", p=128), writes=['VAs'])
                for b4 in range(4):
                    wb = WB[wi % 2]; wk = 'WB%d' % (wi % 2)
                    wa = WA[wi % 2]; wak = 'WA%d' % (wi % 2); wi += 1
                    S.dma('pool', wb, w_b[l, :, b4 * 512:(b4 + 1) * 512].rearrange("(kc p) n -> p kc n", p=128),
                          writes=[wk])
                    S.dma('pool', wa, w_a[l, :, b4 * 512:(b4 + 1) * 512].rearrange("(kc p) n -> p kc n", p=128),
                          writes=[wak])
                    for sub in range(4):
                        fo = b4 * 4 + sub
                        ga = GAt[gi % 2]; gb = GBt[gi % 2]; gak = 'GA%d' % (gi % 2); gbk = 'GB%d_' % (gi % 2); gi += 1
                        S.dma('sp', ga, ZT[5152 + fo * 128:5152 + (fo + 1) * 128, g * 512:(g + 1) * 512], writes=[gak])
                        S.dma('act', gb, ZT[7200 + fo * 128:7200 + (fo + 1) * 128, g * 512:(g + 1) * 512], writes=[gbk])
                        S.op('act', lambda e, ga=ga: e.activation(out=ga, in_=ga, func=AF.Sigmoid), [gak], [gak])
                        S.op('act', lambda e, gb=gb: e.activation(out=gb, in_=gb, func=AF.Sigmoid), [gbk], [gbk])
                        pa = 0 + (fo % 2); pb = 2 + (fo % 2)
                        for kc in range(8):
                            S.op('pe', lambda e, kc=kc, wa=wa, sub=sub, pa=pa: e.matmul(
                                PS[pa][:, :], wa[:, kc, sub * 128:(sub + 1) * 128], VAs[:, kc, :],
                                start=(kc == 0), stop=(kc == 7)), [wak, 'VAs'], [PK[pa]])
                        for kc in range(16):
                            S.op('pe', lambda e, kc=kc, wb=wb, sub=sub, pb=pb: e.matmul(
                                PS[pb][:, :], wb[:, kc, sub * 128:(sub + 1) * 128], HT[:, kc, :],
                                start=(kc == 0), stop=(kc == 15)), [wk, 'HT'], [PK[pb]])
                        S.op('dve', lambda e, ga=ga, pa=pa: e.tensor_tensor(ga, ga, PS[pa][:, :], ALU.mult),
                             [gak, PK[pa]], [gak])
                        S.op('dve', lambda e, gb=gb, pb=pb: e.tensor_tensor(gb, gb, PS[pb][:, :], ALU.mult),
                             [gbk, PK[pb]], [gbk])
                        S.op('pool', lambda e, ga=ga, gb=gb, fo=fo: e.tensor_tensor(YT[:, fo, :], ga, gb, ALU.add),
                             [gak, gbk], ['YT'])
                for t in range(4):
                    tok0 = (g * 4 + t) * 128
                    for cb in range(4):
                        wb = WB[wi % 2]; wk = 'WB%d' % (wi % 2); wi += 1
                        S.dma('pool', wb, w_o[l, :, cb * 512:(cb + 1) * 512].rearrange("(kc p) n -> p kc n", p=128),
                              writes=[wk])
                        for kc in range(16):
                            S.op('pe', lambda e, kc=kc, wb=wb, cb=cb, t=t: e.matmul(
                                PS[4 + cb][:, :], YT[:, kc, t * 128:(t + 1) * 128], wb[:, kc, :],
                                start=(kc == 0), stop=(kc == 15)), [wk, 'YT'], [PK[4 + cb]])
                    post_ln(l, 0, tok0, cond, [4, 5, 6, 7], xsrc(l, tok0, 128), X1[tok0:tok0 + 128, :],
                            (XT, XN, G1B, LG, LB, small), ['XT', 'XN'])
            S.barrier()
            if stage <= 4:
                continue

            AR.reset()
            UN = [AR.f32([128, 2, D]) for _ in range(2)]
            UT = [AR.bf16([128, 16, 256]) for _ in range(2)]
            for i in range(8):
                S.dma('pool', WVB[i * 2048:(i + 1) * 2048, :], peer_v[l, i * 2048:(i + 1) * 2048, :])
            for eb in range(64):
                un = UN[eb % 2]; unk = 'UN%d' % (eb % 2); ut = UT[eb % 2]; utk = 'UT%d' % (eb % 2)
                S.dma('sp' if eb % 2 else 'act', un,
                      peer_u[l, eb * 256:(eb + 1) * 256, :].rearrange("(a p) f -> p a f", p=128), writes=[unk])
                for a in range(2):
                    for q4 in range(4):
                        b = (a * 4 + q4) % 8
                        for j in range(4):
                            kc = q4 * 4 + j
                            S.op('pe', lambda e, kc=kc, j=j, b=b, a=a, un=un: e.transpose(
                                PS[b][:, j * 128:(j + 1) * 128], un[:, a, kc * 128:(kc + 1) * 128], ident[:]),
                                 [unk, 'ident'], [PK[b]])
                        evac(ut[:, q4 * 4:q4 * 4 + 4, a * 128:(a + 1) * 128],
                             PS[b][:, :].rearrange("p (n d) -> p n d", d=128), [PK[b]], [utk])
                S.dma('sp', WUT[eb], ut.rearrange("p a b -> p (a b)"), reads=[utk])
            S.barrier()

            AR.reset()
            WB = [AR.bf16([128, 16, 512])] * 2
            HT = AR.bf16([128, 16, 512]); QQ = AR.bf16([128, 16, 512])
            XT = [AR.f32([128, D])] * 2; XN = AR.f32([128, D])
            KN = AR.f32([128, 2, 128]); KT = AR.bf16([128, 2, 128])
            SC = [AR.f32([128, 8, 2, 128]) for _ in range(4)]
            TAU = [AR.f32([128, 8]) for _ in range(4)]; NBt = [AR.f32([128, 8]) for _ in range(4)]
            V16 = AR.f32([128, 8, 2, 16]); WK = AR.f32([128, 256]); CAND = AR.f32([128, 8, 256])
            SV = AR.f32([128, 8, 16]); ZS = AR.f32([128, 8]); NEGM = AR.f32([128, 8]); JK = AR.f32([128, 16])
            UTb = [AR.bf16([128, 16, 256]) for _ in range(2)]
            VBb = [AR.bf16([128, 2, D]) for _ in range(2)]
            GE = [AR.f32([128, 256]) for _ in range(2)]
            TT = [AR.f32([128, 8, 256])] * 2
            EX = [AR.bf16([128, 8, 256]) for _ in range(2)]
            MK = AR.bf16([128, 8, 256]); GG = AR.f32([128, 256]); GAs = AR.f32([128, 256])
            GATt = [AR.bf16([128, 2, 128]) for _ in range(2)]
            G2B = AR.f32([128, D]); LG = AR.f32([128, D]); LB = AR.f32([128, D])
            S.dma('sp', LG, ln_g[l, 1:2, :].broadcast_to([128, D]), writes=['LG'])
            S.dma('sp', LB, ln_b[l, 1:2, :].broadcast_to([128, D]), writes=['LB'])
            S.dma('sp', KN, pkeys[l].rearrange("a n d -> n a d"), writes=['KN'])
            for a in range(2):
                S.op('pe', lambda e, a=a: e.transpose(PS[7][:, a * 128:(a + 1) * 128], KN[:, a, :], ident[:]),
                     ['KN', 'ident'], [PK[7]])
            S.op('dve', lambda e: e.tensor_copy(KT, PS[7][:, 0:256].rearrange("p (a n) -> p a n", a=2)), [PK[7]], ['KT'])
            wi = 0; ei = 0
            for g in range(9):
                cond = 0 if g < 8 else 1
                if g == 0 or g == 8:
                    S.dma('sp', G2B, MOD[cond:cond + 1, 5 * D:6 * D].broadcast_to([128, D]), writes=['GB'])
                for t in range(4):
                    tok0 = (g * 4 + t) * 128
                    ln_transpose(X1[tok0:tok0 + 128, :], XT[0], 'XT0', XN, 'XN', small, 'small',
                                 HT, 'HT', t * 128, cond, 2, 3, [4, 5, 6, 7])
                for b4 in range(4):
                    wb = WB[0]; wk = 'WB0'; wi += 1
                    S.dma('pool', wb, w_pq[l, :, b4 * 512:(b4 + 1) * 512].rearrange("(kc p) n -> p kc n", p=128),
                          writes=[wk])
                    for sub in range(4):
                        j = b4 * 4 + sub
                        b = 4 + (j % 4)
                        for kc in range(16):
                            S.op('pe', lambda e, kc=kc, wb=wb, sub=sub, b=b: e.matmul(
                                PS[b][:, :], wb[:, kc, sub * 128:(sub + 1) * 128], HT[:, kc, :],
                                start=(kc == 0), stop=(kc == 15)), [wk, 'HT'], [PK[b]])
                        evac(QQ[:, j, :], PS[b][:, :], [PK[b]], ['QQ'])
                for t in range(4):
                    sc = SC[t]; sck = 'SC%d' % t
                    for q4 in range(4):
                        b = 4 + q4
                        for jj in range(4):
                            j = q4 * 4 + jj
                            S.op('pe', lambda e, j=j, jj=jj, b=b, t=t: e.matmul(
                                PS[b][:, jj * 128:(jj + 1) * 128], QQ[:, j, t * 128:(t + 1) * 128], KT[:, j % 2, :],
                                start=True, stop=True), ['QQ', 'KT'], [PK[b]])
                        evac(sc.rearrange("p h a n -> p (h a n)")[:, q4 * 512:(q4 + 1) * 512], PS[b][:, :], [PK[b]], [sck])
                    for h in range(8):
                        for a in range(2):
                            S.op('dve', lambda e, h=h, a=a, sc=sc: e.max(V16[:, h, a, 0:8], sc[:, h, a, :]), [sck], ['V16'])
                            S.op('dve', lambda e, h=h, a=a, sc=sc: e.match_replace(WK[:, 0:128], V16[:, h, a, 0:8],
                                                                                  sc[:, h, a, :], -1e30),
                                 [sck, 'V16'], ['WK'])
                            S.op('dve', lambda e, h=h, a=a: e.max(V16[:, h, a, 8:16], WK[:, 0:128]), ['WK'], ['V16'])
                    C4 = CAND.rearrange("p h (i j) -> p h i j", i=16)
                    S.op('dve', lambda e, C4=C4: e.tensor_tensor(
                        C4, V16[:, :, 0, :].unsqueeze(3).broadcast_to([128, 8, 16, 16]),
                        V16[:, :, 1, :].unsqueeze(2).broadcast_to([128, 8, 16, 16]), ALU.add), ['V16'], ['CAND'])
                    for h in range(8):
                        S.op('dve', lambda e, h=h: e.max(SV[:, h, 0:8], CAND[:, h, :]), ['CAND'], ['SV'])
                        S.op('dve', lambda e, h=h: e.match_replace(WK[:, :], SV[:, h, 0:8], CAND[:, h, :], -1e30),
                             ['CAND', 'SV'], ['WK'])
                        S.op('dve', lambda e, h=h: e.max(SV[:, h, 8:16], WK[:, :]), ['WK'], ['SV'])
                    S.op('dve', lambda e, t=t: e.tensor_copy(TAU[t], SV[:, :, 15]), ['SV'], ['TAU%d' % t])
                    S.op('dve', lambda e: e.tensor_scalar(NEGM, SV[:, :, 0], -1.0, None, ALU.mult), ['SV'], ['NEGM'])
                    for h in range(8):
                        S.op('act', lambda e, h=h: e.activation(out=JK, in_=SV[:, h, :], func=AF.Exp,
                                                                bias=NEGM[:, h:h + 1], scale=1.0,
                                                                accum_out=ZS[:, h:h + 1]), ['SV', 'NEGM'], ['JK', 'ZS'])
                    S.op('act', lambda e: e.activation(out=ZS, in_=ZS, func=AF.Ln), ['ZS'], ['ZS'])
                    S.op('dve', lambda e, t=t: e.tensor_tensor(NBt[t], NEGM, ZS, ALU.subtract), ['NEGM', 'ZS'],
                         ['NB%d' % t])
                for t in range(4):
                    tok0 = (g * 4 + t) * 128
                    sc = SC[t]; sck = 'SC%d' % t
                    for eb in range(64):
                        p2 = ei % 2; ei += 1
                        utb = UTb[p2]; vbb = VBb[p2]; ge = GE[p2]; tt = TT[p2]; ex = EX[p2]; gat = GATt[p2]
                        S.dma('sp', utb.rearrange("p a b -> p (a b)"), WUT[eb], writes=['UTb%d' % p2])
                        S.dma('act', vbb, WVB[eb * 256:(eb + 1) * 256, :].rearrange("(a p) f -> p a f", p=128),
                              writes=['VBb%d' % p2])
                        pa = 4 + p2
                        for kc in range(16):
                            S.op('pe', lambda e, kc=kc, utb=utb, pa=pa, t=t: e.matmul(
                                PS[pa][:, 0:256], HT[:, kc, t * 128:(t + 1) * 128], utb[:, kc, :],
                                start=(kc == 0), stop=(kc == 15)), ['HT', 'UTb%d' % p2], [PK[pa]])
                        S.op('act', lambda e, ge=ge, pa=pa: e.activation(out=ge, in_=PS[pa][:, 0:256], func=AF.Square),
                             [PK[pa]], ['GE%d' % p2])
                        S.op('pool', lambda e, ge=ge: e.tensor_scalar(ge, ge, 0.044715 * GELU_C, GELU_C, ALU.mult,
                                                                      ALU.add), ['GE%d' % p2], ['GE%d' % p2])
                        S.op('dve', lambda e, ge=ge, pa=pa: e.tensor_tensor(ge, ge, PS[pa][:, 0:256], ALU.mult),
                             ['GE%d' % p2, PK[pa]], ['GE%d' % p2])
                        S.op('act', lambda e, ge=ge: e.activation(out=ge, in_=ge, func=AF.Sigmoid), ['GE%d' % p2],
                             ['GE%d' % p2])
                        t4 = tt.rearrange("p h (a n) -> p h a n", a=2)
                        S.op('pool', lambda e, t4=t4, sc=sc, eb=eb: e.tensor_tensor(
                            t4, sc[:, :, 0, eb * 2:eb * 2 + 2].unsqueeze(3).broadcast_to([128, 8, 2, 128]),
                            sc[:, :, 1, :].unsqueeze(2).broadcast_to([128, 8, 2, 128]), ALU.add), [sck], ['TT0'])
                        for h in range(8):
                            S.op('act', lambda e, h=h, ex=ex, tt=tt, t=t: e.activation(
                                out=ex[:, h, :], in_=tt[:, h, :], func=AF.Exp, bias=NBt[t][:, h:h + 1], scale=1.0),
                                 ['TT0', 'NB%d' % t], ['EX%d' % p2])
                        S.op('dve', lambda e, tt=tt, t=t: e.tensor_tensor(
                            MK, tt, TAU[t].unsqueeze(2).broadcast_to([128, 8, 256]), ALU.is_ge),
                             ['TT0', 'TAU%d' % t], ['MK'])
                        S.op('dve', lambda e, ex=ex: e.tensor_tensor(MK, MK, ex, ALU.mult), ['MK', 'EX%d' % p2], ['MK'])
                        S.op('dve', lambda e: e.tensor_reduce(GG, MK.rearrange("p h e -> p e h"), AX.X, ALU.add),
                             ['MK'], ['GG'])
                        S.op('dve', lambda e, ge=ge: e.tensor_tensor(GG, GG, ge, ALU.mult), ['GG', 'GE%d' % p2], ['GG'])
                        S.op('dve', lambda e, pa=pa: e.tensor_tensor(GAs, GG, PS[pa][:, 0:256], ALU.mult),
                             ['GG', PK[pa]], ['GAs'])
                        pt = 6 + p2
                        for a in range(2):
                            S.op('pe', lambda e, a=a, pt=pt: e.transpose(PS[pt][:, a * 128:(a + 1) * 128],
                                                                         GAs[:, a * 128:(a + 1) * 128], ident[:]),
                                 ['GAs', 'ident'], [PK[pt]])
                        S.op('act', lambda e, gat=gat, pt=pt: e.activation(
                            out=gat, in_=PS[pt][:, 0:256].rearrange("p (a n) -> p a n", a=2), func=AF.Copy),
                             [PK[pt]], ['GAT%d' % p2])
                        for a in range(2):
                            for cb in range(4):
                                S.op('pe', lambda e, a=a, cb=cb, gat=gat, vbb=vbb, eb=eb: e.matmul(
                                    PS[cb][:, :], gat[:, a, :], vbb[:, a, cb * 512:(cb + 1) * 512],
                                    start=(eb == 0 and a == 0), stop=(eb == 63 and a == 1)),
                                     ['GAT%d' % p2, 'VBb%d' % p2], [PK[cb]])
                    post_ln(l, 1, tok0, cond, [0, 1, 2, 3], X1[tok0:tok0 + 128, :], xdst(l, tok0, 128),
                            (XT[0], XN, G2B, LG, LB, small), ['XT0', 'XN'])
            S.barrier()
        S.barrier()
    return nc


_CACHE = {}


def kernel(**inp):
    stage = int(os.environ.get("MK_STAGE", "99"))
    if stage not in _CACHE:
        _CACHE[stage] = build(stage)
    nc = _CACHE[stage]
    f = lambda a: np.ascontiguousarray(np.asarray(a, dtype=np.float32))
    ident = np.eye(128, dtype=np.float32)
    s_, c_ = np.meshgrid(np.arange(64), np.arange(64), indexing='ij')
    mfw = (c_ >= s_).astype(np.float32)
    mbw = (c_ <= s_).astype(np.float32)
    rm = np.ones((128, 512), np.float32); rm[:, ::64] = 0.0
    shared = {k: f(inp[k]) for k in ("w_in", "w_conv", "w_a", "w_gk_up", "b_gk", "w_gla_norm", "w_b", "w_o", "w_ada",
                                      "b_ada", "ln_g", "ln_b", "w_pq", "peer_keys", "peer_u", "peer_v")}
    shared.update(c_ident=ident, c_mf=mfw, c_mb=mbw, c_rm=rm)
    x_prompt = f(inp["x_prompt"]); x_sample = f(inp["x_sample"]); sg = f(inp["state_gla"])
    c = f(inp["c"]); c_ctx = f(inp["c_ctx"])
    in_maps = []
    for i in range(8):
        m = dict(shared)
        m["xs"] = x_sample[i]
        m["xp"] = np.ascontiguousarray(x_prompt[2 * i:2 * i + 2].reshape(512, D))
        m["st0"] = np.ascontiguousarray(sg[i])
        m["cvec"] = np.ascontiguousarray(np.stack([c[i], c_ctx], 0))
        in_maps.append(m)
    ncore = int(os.environ.get("MK_NCORE", "8"))
    res = run_bass_kernel_spmd(nc, in_maps[:ncore], core_ids=list(range(ncore)))
    R = res.results
    if ncore < 8:
        return R
    y_sample = np.stack([R[i]["ys"] for i in range(8)], 0).astype(np.float32)
    y_prompt = np.concatenate([R[i]["yp"].reshape(2, 256, D) for i in range(8)], 0).astype(np.float32)
    new_state = np.concatenate([R[i]["ns"] for i in range(8)], 0).astype(np.float32)
    return (y_prompt, y_sample, new_state)
```

```python
import contextlib
import os
import numpy as np
import concourse.bass as bass
import concourse.mybir as mybir
from concourse.bass_utils import run_bass_kernel_spmd

F32 = mybir.dt.float32
BF16 = mybir.dt.bfloat16
AF = mybir.ActivationFunctionType
ALU = mybir.AluOpType
AX = mybir.AxisListType

D = 2048
DEPTH = 2
NTOK = 4608
TS = 4096
NT = NTOK // 128
DIN = 13344
ALPHA = (2.0 * DEPTH) ** 0.25
EPS = 1e-5
NSD = 8
GELU_C = 1.5957691216057308


class Sched:
    def __init__(self, nc, es):
        self.nc = nc
        self.eng = {'pe': nc.tensor, 'act': nc.scalar, 'dve': nc.vector, 'pool': nc.gpsimd, 'sp': nc.sync}
        self.sem = {e: es.enter_context(nc.semaphore('s_' + e)) for e in ('pe', 'act', 'dve', 'pool')}
        self.cnt = {e: 0 for e in self.sem}
        self.waited = {e: {} for e in self.eng}
        self.dsem = {q: [es.enter_context(nc.semaphore('d_%s%d' % (q, i))) for i in range(NSD)]
                     for q in ('sp', 'pool', 'act')}
        self.dcnt = {q: 0 for q in self.dsem}
        self.lastw = {}
        self.readers = {}

    def _wait(self, e, ev):
        key, val, semh = ev
        if e == 'pe' and key == ('c', 'pe'):
            return
        if self.waited[e].get(key, 0) < val:
            self.eng[e].wait_ge(semh, val)
            self.waited[e][key] = val

    def _deps(self, e, reads, writes):
        for k in reads:
            w = self.lastw.get(k)
            if w is not None:
                self._wait(e, w)
        for k in writes:
            w = self.lastw.get(k)
            if w is not None:
                self._wait(e, w)
            for r in self.readers.get(k, {}).values():
                self._wait(e, r)

    def _reg(self, ev, reads, writes):
        for k in reads:
            self.readers.setdefault(k, {})[ev[0]] = ev
        for k in writes:
            self.lastw[k] = ev
            self.readers[k] = {}

    def op(self, e, fn, reads=(), writes=()):
        self._deps(e, reads, writes)
        ins = fn(self.eng[e])
        self.cnt[e] += 1
        ins.then_inc(self.sem[e], 1)
        self._reg((('c', e), self.cnt[e], self.sem[e]), reads, writes)

    def dma(self, q, out, in_, reads=(), writes=(), slow=False):
        i = self.dcnt[q]
        self.dcnt[q] += 1
        slot = i % NSD
        val = 16 * (i // NSD + 1)
        semh = self.dsem[q][slot]
        key = ('d', q, slot)
        if val > 16:
            self._wait(q, (key, val - 16, semh))
        self._deps(q, reads, writes)
        if slow:
            ins = self.eng[q].dma_start(out=out, in_=in_, allow_slow_non_contiguous=True)
        else:
            ins = self.eng[q].dma_start(out=out, in_=in_)
        ins.then_inc(semh, 16)
        self._reg((key, val, semh), reads, writes)

    def barrier(self):
        evs = []
        for e in self.sem:
            if self.cnt[e] > 0:
                evs.append((('c', e), self.cnt[e], self.sem[e]))
        for q in self.dsem:
            n = self.dcnt[q]
            for slot in range(NSD):
                if n > slot:
                    k = (n - 1 - slot) // NSD + 1
                    evs.append((('d', q, slot), 16 * k, self.dsem[q][slot]))
        for e in self.eng:
            for ev in evs:
                key, val, semh = ev
                if self.waited[e].get(key, 0) < val:
                    self.eng[e].wait_ge(semh, val)
                    self.waited[e][key] = val


class Arena:
    def __init__(self, ar, nwords):
        self.ar = ar
        self.n = nwords
        self.off = 0

    def reset(self):
        self.off = 0

    def f32(self, shape):
        n = int(np.prod(shape[1:]))
        assert self.off + n <= self.n, ("arena overflow", self.off, n)
        v = self.ar[0:shape[0], self.off:self.off + n]
        self.off += n
        return _shape(v, shape)

    def bf16(self, shape):
        n = int(np.prod(shape[1:]))
        nw = (n + 1) // 2
        assert self.off + nw <= self.n, ("arena overflow", self.off, nw)
        v = self.ar[0:shape[0], self.off:self.off + nw].bitcast(BF16)[:, 0:n]
        self.off += nw
        return _shape(v, shape)


def _shape(v, shape):
    if len(shape) == 2:
        return v
    if len(shape) == 3:
        return v.rearrange("p (a b) -> p a b", a=shape[1])
    if len(shape) == 4:
        return v.rearrange("p (a b c) -> p a b c", a=shape[1], b=shape[2])
    raise ValueError(shape)


def build(stage=99):
    nc = bass.Bass("TRN2", target_bir_lowering=False)

    def din(name, shape, dt=F32):
        return nc.dram_tensor(name, list(shape), dt, kind="ExternalInput").ap()

    def dout(name, shape, dt=F32):
        return nc.dram_tensor(name, list(shape), dt, kind="ExternalOutput").ap()

    def dscr(name, shape, dt=F32):
        kind = "ExternalOutput" if os.environ.get("MK_DEBUG") else "Internal"
        return nc.dram_tensor(name, list(shape), dt, kind=kind).ap()

    xs = din("xs", [TS, D]); xp = din("xp", [512, D])
    st0 = din("st0", [DEPTH, 2, 4, 256, 512])
    cvec = din("cvec", [2, D])
    w_in = din("w_in", [DEPTH, D, DIN]); w_conv = din("w_conv", [DEPTH, 3, 1024])
    w_a = din("w_a", [DEPTH, 1024, D]); w_gk = din("w_gk_up", [DEPTH, 2, 16, 1024])
    b_gk = din("b_gk", [DEPTH, 2, 1024]); w_gn = din("w_gla_norm", [DEPTH, D])
    w_b = din("w_b", [DEPTH, D, D]); w_o = din("w_o", [DEPTH, D, D])
    w_ada = din("w_ada", [DEPTH, D, 6 * D]); b_ada = din("b_ada", [DEPTH, 6 * D])
    ln_g = din("ln_g", [DEPTH, 2, D]); ln_b = din("ln_b", [DEPTH, 2, D])
    w_pq = din("w_pq", [DEPTH, D, D]); pkeys = din("peer_keys", [DEPTH, 2, 128, 128])
    peer_u = din("peer_u", [DEPTH, 16384, D]); peer_v = din("peer_v", [DEPTH, 16384, D])
    c_ident = din("c_ident", [128, 128]); c_mf = din("c_mf", [64, 64]); c_mb = din("c_mb", [64, 64])
    c_rm = din("c_rm", [128, 512])

    ys = dout("ys", [TS, D]); yp = dout("yp", [512, D])
    ns = dout("ns", [2, DEPTH, 2, 4, 256, 512])

    ZT = dscr("ZT", [9248, NTOK]); Z = dscr("Z", [NTOK, 4096])
    VAT = dscr("VAT", [1024, NTOK], BF16)
    OF = dscr("OF", [NTOK, D]); OB = dscr("OB", [NTOK, D])
    X1 = dscr("X1", [NTOK, D]); X2 = dscr("X2", [NTOK, D])
    MOD = dscr("MOD", [2, 6 * D])
    WUT = dscr("WUT", [32, 128, 8192], BF16); WVB = dscr("WVB", [16384, D], BF16)

    es = contextlib.ExitStack()
    with es:
        S = Sched(nc, es)
        NW = 51900
        ARt = es.enter_context(nc.sbuf_tensor("arena", [128, NW], F32))
        AR = Arena(ARt, NW)
        ident = es.enter_context(nc.sbuf_tensor("ident", [128, 128], F32))
        mf = es.enter_context(nc.sbuf_tensor("mf", [64, 64], F32))
        mb = es.enter_context(nc.sbuf_tensor("mb", [64, 64], F32))
        rmk = es.enter_context(nc.sbuf_tensor("rmk", [128, 512], F32))
        modT = es.enter_context(nc.sbuf_tensor("modT", [128, 2, 4, 16], F32))
        small = es.enter_context(nc.sbuf_tensor("small", [128, 256], F32))
        PS = [es.enter_context(nc.psum_tensor("ps%d" % i, [128, 512], F32)) for i in range(8)]
        PK = ["ps%d" % i for i in range(8)]

        S.dma('sp', ident[:], c_ident[:, :], writes=['ident'])
        S.dma('sp', mf[:], c_mf[:, :], writes=['mf'])
        S.dma('sp', mb[:], c_mb[:, :], writes=['mb'])
        S.dma('sp', rmk[:], c_rm[:, :], writes=['rmk'])

        def xsrc(l, t0, n):
            if l == 0:
                return xs[t0:t0 + n, :] if t0 < TS else xp[t0 - TS:t0 - TS + n, :]
            return X2[t0:t0 + n, :]

        def xdst(l, t0, n):
            if l == DEPTH - 1:
                return ys[t0:t0 + n, :] if t0 < TS else yp[t0 - TS:t0 - TS + n, :]
            return X2[t0:t0 + n, :]

        rr = {'ev': 0}

        def evac(out, in_, reads, writes):
            rr['ev'] += 1
            if rr['ev'] % 2:
                S.op('act', lambda e: e.activation(out=out, in_=in_, func=AF.Copy), reads, writes)
            else:
                S.op('dve', lambda e: e.tensor_copy(out, in_), reads, writes)

        def ln_stats(xt, xk, sm, smk):
            for j in range(4):
                S.op('dve', lambda e, j=j: e.bn_stats(sm[:, j * 6:(j + 1) * 6], xt[:, j * 512:(j + 1) * 512]),
                     [xk], [smk])
            S.op('dve', lambda e: e.bn_aggr(sm[:, 24:26], sm[:, 0:24]), [smk], [smk])
            S.op('act', lambda e: e.activation(out=sm[:, 26:27], in_=sm[:, 25:26], func=AF.Sqrt, bias=EPS, scale=1.0),
                 [smk], [smk])
            S.op('dve', lambda e: e.reciprocal(sm[:, 27:28], sm[:, 26:27]), [smk], [smk])
            S.op('dve', lambda e: e.scalar_tensor_tensor(out=sm[:, 28:29], in0=sm[:, 24:25], scalar=-1.0,
                                                          in1=sm[:, 27:28], op0=ALU.mult, op1=ALU.mult),
                 [smk], [smk])
            return sm[:, 27:28], sm[:, 28:29]

        def ln_transpose(src_ap, xt, xk, xn, xnk, sm, smk, HT, hk, col0, cond, jsh, jsc, psb):
            S.dma('sp', xt, src_ap, writes=[xk])
            rstd, nb = ln_stats(xt, xk, sm, smk)
            S.op('act', lambda e: e.activation(out=xn, in_=xt, func=AF.Identity, bias=nb, scale=rstd),
                 [xk, smk], [xnk])
            for q4 in range(4):
                b = psb[q4 % len(psb)]
                for j in range(4):
                    kc = q4 * 4 + j
                    S.op('pe', lambda e, kc=kc, j=j, b=b: e.transpose(PS[b][:, j * 128:(j + 1) * 128],
                                                                      xn[:, kc * 128:(kc + 1) * 128], ident[:]),
                         [xnk, 'ident'], [PK[b]])
                for j in range(4):
                    kc = q4 * 4 + j
                    o_ = HT[:, kc, col0:col0 + 128]
                    i_ = PS[b][:, j * 128:(j + 1) * 128]
                    if cond is None:
                        evac(o_, i_, [PK[b]], [hk])
                    elif kc % 2:
                        S.op('dve', lambda e, o_=o_, i_=i_, kc=kc: e.tensor_scalar(
                            o_, i_, modT[:, cond, jsc, kc:kc + 1], modT[:, cond, jsh, kc:kc + 1], ALU.mult, ALU.add),
                             [PK[b], 'modT'], [hk])
                    else:
                        S.op('act', lambda e, o_=o_, i_=i_, kc=kc: e.activation(
                            out=o_, in_=i_, func=AF.Identity, bias=modT[:, cond, jsh, kc:kc + 1],
                            scale=modT[:, cond, jsc, kc:kc + 1]), [PK[b], 'modT'], [hk])

        def post_ln(l, which, t0, cond, acc_tiles, xin_ap, dst_ap, bufs, ek):
            xt, xn, GB_, LG, LB, sm = bufs
            S.dma('sp', xt, xin_ap, writes=['pl_x'] + ek)
            for cb in range(4):
                sl = slice(cb * 512, (cb + 1) * 512)
                S.op('dve', lambda e, cb=cb, sl=sl: e.tensor_tensor(xn[:, sl], PS[acc_tiles[cb]][:, :], GB_[:, sl],
                                                                     ALU.mult),
                     [PK[acc_tiles[cb]], 'GB'], ['pl_n%d' % cb] + ek)
                S.op('dve', lambda e, sl=sl: e.scalar_tensor_tensor(out=xn[:, sl], in0=xt[:, sl], scalar=ALPHA,
                                                                     in1=xn[:, sl], op0=ALU.mult, op1=ALU.add),
                     ['pl_x', 'pl_n%d' % cb], ['pl_n%d' % cb] + ek)
            allk = ['pl_n%d' % cb for cb in range(4)]
            S.op('pool', lambda e: e.tensor_copy(xt, xn), allk, ['pl_x'] + ek)
            rstd, nb = ln_stats(xt, 'pl_x', sm, 'small')
            S.op('act', lambda e: e.activation(out=xn, in_=xt, func=AF.Identity, bias=nb, scale=rstd),
                 ['pl_x', 'small'], allk + ek)
            S.op('dve', lambda e: e.tensor_tensor(xt, xn, LG, ALU.mult), allk + ['LG'], ['pl_x'] + ek)
            S.op('pool', lambda e: e.tensor_tensor(xn, xt, LB, ALU.add), ['pl_x', 'LB'], allk + ek)
            S.dma('sp', dst_ap, xn, reads=allk)

        for l in range(int(os.environ.get("MK_LAYERS", DEPTH))):
            AR.reset()
            CT = AR.f32([128, 2, 16]); BA = AR.f32([2, 6 * D]); MS = AR.f32([2, 6 * D])
            WF = [AR.f32([128, 16, 512]) for _ in range(2)]
            for c in range(2):
                S.dma('sp', CT[:, c, :], cvec[c].rearrange("(kc p) -> p kc", p=128), writes=['CT'], slow=True)
            S.dma('sp', BA, b_ada[l:l + 1, :].broadcast_to([2, 6 * D]), writes=['BA'])
            S.op('act', lambda e: e.activation(out=CT, in_=CT, func=AF.Silu), ['CT'], ['CT'])
            for cb in range(24):
                wf = WF[cb % 2]; wk = 'WF%d' % (cb % 2)
                S.dma('sp' if cb % 2 else 'act', wf,
                      w_ada[l, :, cb * 512:(cb + 1) * 512].rearrange("(kc p) n -> p kc n", p=128), writes=[wk])
                b = cb % 2
                for kc in range(16):
                    S.op('pe', lambda e, kc=kc, wf=wf, b=b: e.matmul(PS[b][0:2, :], CT[:, :, kc], wf[:, kc, :],
                                                                     start=(kc == 0), stop=(kc == 15)),
                         ['CT', wk], [PK[b]])
                S.op('dve', lambda e, cb=cb, b=b: e.tensor_tensor(MS[:, cb * 512:(cb + 1) * 512], PS[b][0:2, :],
                                                                  BA[:, cb * 512:(cb + 1) * 512], ALU.add),
                     [PK[b], 'BA'], ['MS'])
            S.dma('sp', MOD[:, :], MS, reads=['MS'])
            S.barrier()
            for c in range(2):
                for jj, j in enumerate((0, 1, 3, 4)):
                    S.dma('sp', modT[:, c, jj, :], MOD[c, j * D:(j + 1) * D].rearrange("(kc p) -> p kc", p=128),
                          writes=['modT'], slow=True)
            for c in range(2):
                for jj in (1, 3):
                    S.op('dve', lambda e, c=c, jj=jj: e.tensor_scalar_add(modT[:, c, jj, :], modT[:, c, jj, :], 1.0),
                         ['modT'], ['modT'])
            if stage <= 0:
                continue

            AR.reset()
            WB = [AR.bf16([128, 16, 512]) for _ in range(2)]
            HT = AR.bf16([128, 16, 512])
            XT = [AR.f32([128, D]) for _ in range(2)]
            XN = AR.f32([128, D])
            STG = [AR.f32([128, 512]) for _ in range(4)]
            fm_blocks = [(c0, 512, c0) for c0 in range(0, 5120, 512)] + [(9216, 32, 5120)] + \
                        [(9248 + i * 512, 512, 5152 + i * 512) for i in range(8)]
            tm_blocks = [(5120 + i * 512, 512, i * 512) for i in range(8)]
            wi = 0; si = 0; pi = 0
            for g in range(9):
                for t in range(4):
                    tile = g * 4 + t
                    cond = 0 if tile < 32 else 1
                    ln_transpose(xsrc(l, tile * 128, 128), XT[t % 2], 'XT%d' % (t % 2), XN, 'XN', small, 'small',
                                 HT, 'HT', t * 128, cond, 0, 1, [4, 5, 6, 7])
                for (c0, ncol, r0) in fm_blocks:
                    wb = WB[wi % 2]; wk = 'WB%d' % (wi % 2); wi += 1
                    S.dma('pool', wb[:, :, 0:ncol],
                          w_in[l, :, c0:c0 + ncol].rearrange("(kc p) n -> p kc n", p=128), writes=[wk])
                    for sub in range((ncol + 127) // 128):
                        m = min(128, ncol - sub * 128)
                        b = pi % 4; pi += 1
                        for kc in range(16):
                            S.op('pe', lambda e, kc=kc, wb=wb, b=b, m=m, sub=sub: e.matmul(
                                PS[b][0:m, :], wb[:, kc, sub * 128:sub * 128 + m], HT[:, kc, :],
                                start=(kc == 0), stop=(kc == 15)), [wk, 'HT'], [PK[b]])
                        stg = STG[si % 4]; sk = 'STG%d' % (si % 4); si += 1
                        evac(stg[0:m, :], PS[b][0:m, :], [PK[b]], [sk])
                        S.dma('sp', ZT[r0 + sub * 128:r0 + sub * 128 + m, g * 512:(g + 1) * 512], stg[0:m, :],
                              reads=[sk])
                for (c0, ncol, z0) in tm_blocks:
                    wb = WB[wi % 2]; wk = 'WB%d' % (wi % 2); wi += 1
                    S.dma('pool', wb, w_in[l, :, c0:c0 + 512].rearrange("(kc p) n -> p kc n", p=128), writes=[wk])
                    for t in range(4):
                        b = pi % 4; pi += 1
                        for kc in range(16):
                            S.op('pe', lambda e, kc=kc, wb=wb, b=b, t=t: e.matmul(
                                PS[b][:, :], HT[:, kc, t * 128:(t + 1) * 128], wb[:, kc, :],
                                start=(kc == 0), stop=(kc == 15)), [wk, 'HT'], [PK[b]])
                        stg = STG[si % 4]; sk = 'STG%d' % (si % 4); si += 1
                        evac(stg, PS[b][:, :], [PK[b]], [sk])
                        tok0 = (g * 4 + t) * 128
                        S.dma('sp', Z[tok0:tok0 + 128, z0:z0 + 512], stg, reads=[sk])
            S.barrier()
            if stage <= 1:
                continue

            AR.reset()
            WC = AR.f32([128, 3, 8])
            for j in range(3):
                S.dma('sp', WC[:, j, :], w_conv[l, j].rearrange("(b p) -> p b", p=128), writes=['WC'], slow=True)
            CC = AR.f32([128, 4096]); CX = AR.f32([128, 4096]); CB = AR.f32([128, 4096]); CO = AR.f32([128, 4096])
            VA = AR.bf16([128, 4096])
            for (t0, T, latent) in ((0, 4096, True), (4096, 512, False)):
                for blk in range(8):
                    r = blk * 128
                    S.dma('sp', CC[:, 0:T], ZT[1024 + r:1024 + r + 128, t0:t0 + T], writes=['CC'])
                    S.dma('act', CX[:, 0:T], ZT[2048 + r:2048 + r + 128, t0:t0 + T], writes=['CX'])
                    S.dma('sp', CB[:, 0:T], ZT[r:r + 128, t0:t0 + T], writes=['CB'])
                    S.op('pool', lambda e, T=T: e.tensor_tensor(CC[:, 0:T], CC[:, 0:T], CX[:, 0:T], ALU.mult),
                         ['CC', 'CX'], ['CC'])
                    S.op('dve', lambda e, T=T, blk=blk: e.tensor_scalar(CO[:, 0:T], CC[:, 0:T], WC[:, 1, blk:blk + 1], None,
                                                                         ALU.mult), ['CC', 'WC'], ['CO'])
                    if latent and blk < 4:
                        u3 = CC[:, 0:T].rearrange("p (r c) -> p r c", c=64)
                        o3 = CO[:, 0:T].rearrange("p (r c) -> p r c", c=64)
                        pairs = [(o3[:, :, 1:64], u3[:, :, 0:63], 0), (o3[:, :, 0:63], u3[:, :, 1:64], 2)]
                    elif latent:
                        pairs = [(CO[:, 64:T], CC[:, 0:T - 64], 0), (CO[:, 0:T - 64], CC[:, 64:T], 2)]
                    else:
                        u3 = CC[:, 0:T].rearrange("p (r c) -> p r c", c=256)
                        o3 = CO[:, 0:T].rearrange("p (r c) -> p r c", c=256)
                        pairs = [(o3[:, :, 1:256], u3[:, :, 0:255], 0), (o3[:, :, 0:255], u3[:, :, 1:256], 2)]
                    for (oo, uu, j) in pairs:
                        S.op('dve', lambda e, oo=oo, uu=uu, j=j, blk=blk: e.scalar_tensor_tensor(
                            out=oo, in0=uu, scalar=WC[:, j, blk:blk + 1], in1=oo, op0=ALU.mult, op1=ALU.add),
                             ['CC', 'CO', 'WC'], ['CO'])
                    S.op('pool', lambda e, T=T: e.tensor_tensor(VA[:, 0:T], CB[:, 0:T], CO[:, 0:T], ALU.mult),
                         ['CB', 'CO'], ['VA'])
                    S.dma('sp', VAT[r:r + 128, t0:t0 + T], VA[:, 0:T], reads=['VA'])
            S.barrier()
            if stage <= 2:
                continue

            for dr in range(2):
                AR.reset()
                OD = OF if dr == 0 else OB
                msk = mf if dr == 0 else mb
                mk = 'mf' if dr == 0 else 'mb'
                WGf = AR.f32([16, 1024]); WG = AR.bf16([16, 1024]); NBG = AR.f32([128, 8])
                S.dma('sp', WGf, w_gk[l, dr], writes=['WGf'])
                S.op('dve', lambda e: e.tensor_copy(WG, WGf), ['WGf'], ['WG'])
                S.dma('sp', NBG, b_gk[l, dr].rearrange("(j p) -> p j", p=128), writes=['NBG'], slow=True)
                S.op('dve', lambda e: e.tensor_scalar(NBG, NBG, -1.0, None, ALU.mult), ['NBG'], ['NBG'])
                LFf = AR.f32([16, 512]); LFb = AR.bf16([16, 512])
                Sst = [AR.f32([128, 2, 512]) for _ in range(4)]
                Sbf = [AR.bf16([128, 2, 512]) for _ in range(4)]
                QK = [AR.f32([128, 2, 2, 512]) for _ in range(2)]
                Gt = AR.f32([128, 512]); Bt = AR.f32([128, 512]); B2 = AR.f32([128, 512]); Et = AR.f32([128, 512])
                DEC = [AR.f32([128, 2, 8]) for _ in range(4)]
                QE = [AR.bf16([128, 2, 512]) for _ in range(4)]
                KE = [AR.bf16([128, 2, 512]) for _ in range(4)]
                KD32 = AR.f32([128, 512])
                KDT = [AR.bf16([64, 8, 256]) for _ in range(4)]
                Vt = [AR.bf16([64, 8, 512]) for _ in range(4)]
                AM = [AR.bf16([64, 64]) for _ in range(4)]
                OST = [AR.f32([64, 512]) for _ in range(4)]
                seqs = [(0, 4096, 512, None), (4096, 256, 256, 0), (4352, 256, 256, 1)]
                for (s0, T, BT, pbi) in seqs:
                    nch = BT // 64
                    for h in range(4):
                        if pbi is None:
                            S.dma('sp', Sst[h], st0[l, dr, h].rearrange("(kc p) e -> p kc e", p=128),
                                  writes=['S%d' % h])
                        else:
                            S.op('pool', lambda e, h=h: e.memset(Sst[h], 0.0), [], ['S%d' % h])
                        S.op('pool', lambda e, h=h: e.tensor_copy(Sbf[h], Sst[h]), ['S%d' % h], ['Sb%d' % h])
                    nblk = T // BT
                    order = range(nblk) if dr == 0 else range(nblk - 1, -1, -1)
                    for blk in order:
                        t0 = s0 + blk * BT
                        S.dma('sp', LFf[:, 0:BT], ZT[5120 + dr * 16:5136 + dr * 16, t0:t0 + BT], writes=['LFf'])
                        S.op('dve', lambda e, BT=BT: e.tensor_copy(LFb[:, 0:BT], LFf[:, 0:BT]), ['LFf'], ['LFb'])
                        for h in range(4):
                            qk = QK[h % 2]; qkk = 'QK%d' % (h % 2)
                            S.dma('sp', qk[:, 0, :, 0:BT],
                                  ZT[3072 + h * 256:3072 + (h + 1) * 256, t0:t0 + BT].rearrange("(kc p) t -> p kc t",
                                                                                                 p=128),
                                  writes=[qkk])
                            S.dma('act', qk[:, 1, :, 0:BT],
                                  ZT[4096 + h * 256:4096 + (h + 1) * 256, t0:t0 + BT].rearrange("(kc p) t -> p kc t",
                                                                                                 p=128),
                                  writes=[qkk])
                            S.dma('pool', Vt[h][:, 0:nch, :],
                                  Z[t0:t0 + BT, h * 512:(h + 1) * 512].rearrange("(n c) e -> c n e", c=64),
                                  writes=['V%d' % h])
                            for kc in range(2):
                                j = h * 2 + kc
                                b = 4 + (j % 2)
                                S.op('pe', lambda e, j=j, b=b, BT=BT: e.matmul(
                                    PS[b][:, 0:BT], WG[:, j * 128:(j + 1) * 128], LFb[:, 0:BT], start=True, stop=True),
                                     ['WG', 'LFb'], [PK[b]])
                                S.op('act', lambda e, j=j, b=b, BT=BT: e.activation(
                                    out=Et[:, 0:BT], in_=PS[b][:, 0:BT], func=AF.Exp, bias=NBG[:, j:j + 1], scale=-1.0),
                                     [PK[b], 'NBG'], ['Et'])
                                S.op('act', lambda e, BT=BT: e.activation(out=Et[:, 0:BT], in_=Et[:, 0:BT], func=AF.Ln,
                                                                          bias=1.0, scale=1.0), ['Et'], ['Et'])
                                S.op('dve', lambda e, BT=BT: e.tensor_scalar(Gt[:, 0:BT], Et[:, 0:BT], -1.0 / 16.0, -1.0,
                                                                             ALU.mult, ALU.max), ['Et'], ['Gt'])
                                S.op('dve', lambda e, BT=BT: e.tensor_tensor_scan(
                                    Bt[:, 0:BT], rmk[:, 0:BT], Gt[:, 0:BT], 0.0, ALU.mult, ALU.add),
                                     ['Gt', 'rmk'], ['Bt'])
                                B3 = Bt[:, 0:BT].rearrange("p (n c) -> p n c", c=64)
                                if dr == 1:
                                    S.op('dve', lambda e, BT=BT: e.tensor_tensor(B2[:, 0:BT], Gt[:, 0:BT], Bt[:, 0:BT],
                                                                                 ALU.subtract), ['Gt', 'Bt'], ['B2'])
                                    B23 = B2[:, 0:BT].rearrange("p (n c) -> p n c", c=64)
                                    S.op('dve', lambda e, B23=B23, B3=B3, nch=nch: e.tensor_tensor(
                                        B23, B23, B3[:, :, 63:64].broadcast_to([128, nch, 64]), ALU.add),
                                         ['B2', 'Bt'], ['B2'])
                                    Bu = B2; Buk = 'B2'; blast = B23[:, :, 0]
                                else:
                                    Bu = Bt; Buk = 'Bt'; blast = B3[:, :, 63]
                                dec = DEC[h][:, kc, 0:nch]
                                S.op('act', lambda e, dec=dec, blast=blast: e.activation(out=dec, in_=blast, func=AF.Exp),
                                     [Buk], ['DEC%d' % h])
                                S.op('act', lambda e, Bu=Bu, BT=BT: e.activation(out=Et[:, 0:BT], in_=Bu[:, 0:BT],
                                                                                  func=AF.Exp), [Buk], ['Et'])
                                S.op('dve', lambda e, h=h, kc=kc, qk=qk, BT=BT: e.scalar_tensor_tensor(
                                    out=QE[h][:, kc, 0:BT], in0=qk[:, 0, kc, 0:BT], scalar=1.0 / 16.0, in1=Et[:, 0:BT],
                                    op0=ALU.mult, op1=ALU.mult), [qkk, 'Et'], ['QE%d' % h])
                                S.op('act', lambda e, Bu=Bu, BT=BT: e.activation(out=Et[:, 0:BT], in_=Bu[:, 0:BT],
                                                                                  func=AF.Exp, scale=-1.0), [Buk], ['Et'])
                                S.op('dve', lambda e, h=h, kc=kc, qk=qk, BT=BT: e.tensor_tensor(
                                    qk[:, 1, kc, 0:BT], qk[:, 1, kc, 0:BT], Et[:, 0:BT], ALU.mult), [qkk, 'Et'], [qkk])
                                S.op('pool', lambda e, h=h, kc=kc, qk=qk, BT=BT: e.tensor_copy(
                                    KE[h][:, kc, 0:BT], qk[:, 1, kc, 0:BT]), [qkk], ['KE%d' % h])
                                K3 = qk[:, 1, kc, 0:BT].rearrange("p (n c) -> p n c", c=64)
                                D3 = KD32[:, 0:BT].rearrange("p (n c) -> p n c", c=64)
                                S.op('dve', lambda e, K3=K3, D3=D3, dec=dec, nch=nch: e.tensor_tensor(
                                    D3, K3, dec.unsqueeze(2).broadcast_to([128, nch, 64]), ALU.mult),
                                     [qkk, 'DEC%d' % h], ['KD32'])
                                for cq in range(0, nch, 4):
                                    bb = 6 + (cq // 4 + kc) % 2
                                    for c4 in range(4):
                                        ch = cq + c4
                                        S.op('pe', lambda e, ch=ch, c4=c4, bb=bb: e.transpose(
                                            PS[bb][0:64, c4 * 128:(c4 + 1) * 128], KD32[:, ch * 64:(ch + 1) * 64],
                                            ident[:]), ['KD32', 'ident'], [PK[bb]])
                                    evac(KDT[h][:, cq:cq + 4, kc * 128:(kc + 1) * 128],
                                         PS[bb][0:64, :].rearrange("p (n d) -> p n d", d=128), [PK[bb]], ['KDT%d' % h])
                        corder = range(nch) if dr == 0 else range(nch - 1, -1, -1)
                        for ch in corder:
                            cs = slice(ch * 64, (ch + 1) * 64)
                            for h in range(4):
                                S.op('pe', lambda e, h=h, cs=cs: e.matmul(PS[0][0:64, h * 64:(h + 1) * 64],
                                                                          KE[h][:, 0, cs], QE[h][:, 0, cs],
                                                                          start=True, stop=False),
                                     ['KE%d' % h, 'QE%d' % h], ['psA%d' % h])
                                S.op('pe', lambda e, h=h, cs=cs: e.matmul(PS[0][0:64, h * 64:(h + 1) * 64],
                                                                          KE[h][:, 1, cs], QE[h][:, 1, cs],
                                                                          start=False, stop=True),
                                     ['KE%d' % h, 'QE%d' % h], ['psA%d' % h])
                                S.op('dve', lambda e, h=h: e.tensor_tensor(AM[h], PS[0][0:64, h * 64:(h + 1) * 64], msk[:],
                                                                           ALU.mult), ['psA%d' % h, mk], ['AM%d' % h])
                                bo = 1 + (h % 2)
                                S.op('pe', lambda e, h=h, cs=cs, bo=bo: e.matmul(PS[bo][0:64, :], QE[h][:, 0, cs],
                                                                                 Sbf[h][:, 0, :], start=True, stop=False),
                                     ['QE%d' % h, 'Sb%d' % h], [PK[bo]])
                                S.op('pe', lambda e, h=h, cs=cs, bo=bo: e.matmul(PS[bo][0:64, :], QE[h][:, 1, cs],
                                                                                 Sbf[h][:, 1, :], start=False, stop=False),
                                     ['QE%d' % h, 'Sb%d' % h], [PK[bo]])
                                S.op('pe', lambda e, h=h, ch=ch, bo=bo: e.matmul(PS[bo][0:64, :], AM[h][:, :],
                                                                                 Vt[h][:, ch, :], start=False, stop=True),
                                     ['AM%d' % h, 'V%d' % h], [PK[bo]])
                                S.op('act', lambda e, h=h, bo=bo: e.activation(out=OST[h], in_=PS[bo][0:64, :],
                                                                               func=AF.Copy), [PK[bo]], ['OST%d' % h])
                                tk = t0 + ch * 64
                                S.dma('sp', OD[tk:tk + 64, h * 512:(h + 1) * 512], OST[h], reads=['OST%d' % h])
                                for kc in range(2):
                                    bs = 3 + ((h * 2 + kc) % 4) if False else (3 + (h * 2 + kc) % 2)
                                    S.op('pe', lambda e, h=h, ch=ch, kc=kc, bs=bs: e.matmul(
                                        PS[bs][:, :], KDT[h][:, ch, kc * 128:(kc + 1) * 128], Vt[h][:, ch, :],
                                        start=True, stop=True), ['KDT%d' % h, 'V%d' % h], [PK[bs]])
                                    S.op('dve', lambda e, h=h, ch=ch, kc=kc, bs=bs: e.scalar_tensor_tensor(
                                        out=Sst[h][:, kc, :], in0=Sst[h][:, kc, :], scalar=DEC[h][:, kc, ch:ch + 1],
                                        in1=PS[bs][:, :], op0=ALU.mult, op1=ALU.add),
                                         [PK[bs], 'DEC%d' % h, 'S%d' % h], ['S%d' % h])
                                S.op('pool', lambda e, h=h: e.tensor_copy(Sbf[h], Sst[h]), ['S%d' % h], ['Sb%d' % h])
                    if pbi is not None:
                        for h in range(4):
                            S.dma('sp', ns[pbi, l, dr, h].rearrange("(kc p) e -> p kc e", p=128), Sst[h],
                                  reads=['S%d' % h])
                S.barrier()
            if stage <= 3:
                continue

            AR.reset()
            WB = [AR.bf16([128, 16, 512]) for _ in range(2)]
            WA = [AR.bf16([128, 8, 512]) for _ in range(2)]
            HT = AR.bf16([128, 16, 512]); YT = AR.bf16([128, 16, 512]); VAs = AR.bf16([128, 8, 512])
            XT = AR.f32([128, D]); XN = AR.f32([128, D]); XO = AR.f32([128, D])
            WNB = AR.f32([128, D]); G1B = AR.f32([128, D]); LG = AR.f32([128, D]); LB = AR.f32([128, D])
            GAt = [AR.f32([128, 512]) for _ in range(2)]; GBt = [AR.f32([128, 512]) for _ in range(2)]
            S.dma('sp', WNB, w_gn[l:l + 1, :].broadcast_to([128, D]), writes=['WNB'])
            S.dma('sp', LG, ln_g[l, 0:1, :].broadcast_to([128, D]), writes=['LG'])
            S.dma('sp', LB, ln_b[l, 0:1, :].broadcast_to([128, D]), writes=['LB'])
            wi = 0; gi = 0
            for g in range(9):
                cond = 0 if g < 8 else 1
                if g == 0 or g == 8:
                    S.dma('sp', G1B, MOD[cond:cond + 1, 2 * D:3 * D].broadcast_to([128, D]), writes=['GB'])
                for t in range(4):
                    tok0 = (g * 4 + t) * 128
                    S.dma('sp', XT, OF[tok0:tok0 + 128, :], writes=['XT'])
                    S.dma('act', XN, OB[tok0:tok0 + 128, :], writes=['XN'])
                    S.dma('sp', XO, Z[tok0:tok0 + 128, 2048:4096], writes=['XO'])
                    S.op('pool', lambda e: e.tensor_tensor(XT, XT, XN, ALU.add), ['XT', 'XN'], ['XT'])
                    for h in range(4):
                        S.op('act', lambda e, h=h: e.activation(out=XN[:, h * 512:(h + 1) * 512],
                                                                in_=XT[:, h * 512:(h + 1) * 512], func=AF.Square,
                                                                accum_out=small[:, 32 + h:33 + h]),
                             ['XT'], ['XN', 'small'])
                    S.op('act', lambda e: e.activation(out=small[:, 36:40], in_=small[:, 32:36], func=AF.Sqrt, bias=EPS,
                                                       scale=1.0 / 512.0), ['small'], ['small'])
                    S.op('dve', lambda e: e.reciprocal(small[:, 40:44], small[:, 36:40]), ['small'], ['small'])
                    S.op('act', lambda e: e.activation(out=XO, in_=XO, func=AF.Silu), ['XO'], ['XO'])
                    for h in range(4):
                        hs = slice(h * 512, (h + 1) * 512)
                        S.op('dve', lambda e, h=h, hs=hs: e.scalar_tensor_tensor(
                            out=XN[:, hs], in0=XT[:, hs], scalar=small[:, 40 + h:41 + h], in1=WNB[:, hs],
                            op0=ALU.mult, op1=ALU.mult), ['XT', 'small', 'WNB'], ['XN'])
                    S.op('pool', lambda e: e.tensor_tensor(XN, XN, XO, ALU.mult), ['XN', 'XO'], ['XN'])
                    for q4 in range(4):
                        b = 4 + q4
                        for j in range(4):
                            kc = q4 * 4 + j
                            S.op('pe', lambda e, kc=kc, j=j, b=b: e.transpose(PS[b][:, j * 128:(j + 1) * 128],
                                                                              XN[:, kc * 128:(kc + 1) * 128], ident[:]),
                                 ['XN', 'ident'], [PK[b]])
                        evac(HT[:, q4 * 4:q4 * 4 + 4, t * 128:(t + 1) * 128],
                             PS[b][:, :].rearrange("p (n d) -> p n d", d=128), [PK[b]], ['HT'])
                S.dma('sp', VAs, VAT[:, g * 512:(g + 1) * 512].rearrange("(kc p) t -> p kc t", p=128), writes=['VAs'])
                for b4 in range(4):
                    wb = WB[wi % 2]; wk = 'WB%d' % (wi % 2)
                    wa = WA[wi % 2]; wak = 'WA%d' % (wi % 2); wi += 1
                    S.dma('pool', wb, w_b[l, :, b4 * 512:(b4 + 1) * 512].rearrange("(kc p) n -> p kc n", p=128),
                          writes=[wk])
                    S.dma('pool', wa, w_a[l, :, b4 * 512:(b4 + 1) * 512].rearrange("(kc p) n -> p kc n", p=128),
                          writes=[wak])
                    for sub in range(4):
                        fo = b4 * 4 + sub
                        ga = GAt[gi % 2]; gb = GBt[gi % 2]; gak = 'GA%d' % (gi % 2); gbk = 'GB%d_' % (gi % 2); gi += 1
                        S.dma('sp', ga, ZT[5152 + fo * 128:5152 + (fo + 1) * 128, g * 512:(g + 1) * 512], writes=[gak])
                        S.dma('act', gb, ZT[7200 + fo * 128:7200 + (fo + 1) * 128, g * 512:(g + 1) * 512], writes=[gbk])
                        S.op('act', lambda e, ga=ga: e.activation(out=ga, in_=ga, func=AF.Sigmoid), [gak], [gak])
                        S.op('act', lambda e, gb=gb: e.activation(out=gb, in_=gb, func=AF.Sigmoid), [gbk], [gbk])
                        pa = 0 + (fo % 2); pb = 2 + (fo % 2)
                        for kc in range(8):
                            S.op('pe', lambda e, kc=kc, wa=wa, sub=sub, pa=pa: e.matmul(
                                PS[pa][:, :], wa[:, kc, sub * 128:(sub + 1) * 128], VAs[:, kc, :],
                                start=(kc == 0), stop=(kc == 7)), [wak, 'VAs'], [PK[pa]])
                        for kc in range(16):
                            S.op('pe', lambda e, kc=kc, wb=wb, sub=sub, pb=pb: e.matmul(
                                PS[pb][:, :], wb[:, kc, sub * 128:(sub + 1) * 128], HT[:, kc, :],
                                start=(kc == 0), stop=(kc == 15)), [wk, 'HT'], [PK[pb]])
                        S.op('dve', lambda e, ga=ga, pa=pa: e.tensor_tensor(ga, ga, PS[pa][:, :], ALU.mult),
                             [gak, PK[pa]], [gak])
                        S.op('dve', lambda e, gb=gb, pb=pb: e.tensor_tensor(gb, gb, PS[pb][:, :], ALU.mult),
                             [gbk, PK[pb]], [gbk])
                        S.op('pool', lambda e, ga=ga, gb=gb, fo=fo: e.tensor_tensor(YT[:, fo, :], ga, gb, ALU.add),
                             [gak, gbk], ['YT'])
                for t in range(4):
                    tok0 = (g * 4 + t) * 128
                    for cb in range(4):
                        wb = WB[wi % 2]; wk = 'WB%d' % (wi % 2); wi += 1
                        S.dma('pool', wb, w_o[l, :, cb * 512:(cb + 1) * 512].rearrange("(kc p) n -> p kc n", p=128),
                              writes=[wk])
                        for kc in range(16):
                            S.op('pe', lambda e, kc=kc, wb=wb, cb=cb, t=t: e.matmul(
                                PS[4 + cb][:, :], YT[:, kc, t * 128:(t + 1) * 128], wb[:, kc, :],
                                start=(kc == 0), stop=(kc == 15)), [wk, 'YT'], [PK[4 + cb]])
                    post_ln(l, 0, tok0, cond, [4, 5, 6, 7], xsrc(l, tok0, 128), X1[tok0:tok0 + 128, :],
                            (XT, XN, G1B, LG, LB, small), ['XT', 'XN'])
            S.barrier()
            if stage <= 4:
                continue

            AR.reset()
            UN = [AR.f32([128, 4, D]) for _ in range(2)]
            UT = [AR.bf16([128, 16, 512]) for _ in range(2)]
            for i in range(8):
                S.dma('pool', WVB[i * 2048:(i + 1) * 2048, :], peer_v[l, i * 2048:(i + 1) * 2048, :])
            for eb in range(32):
                un = UN[eb % 2]; unk = 'UN%d' % (eb % 2); ut = UT[eb % 2]; utk = 'UT%d' % (eb % 2)
                S.dma('sp' if eb % 2 else 'act', un,
                      peer_u[l, eb * 512:(eb + 1) * 512, :].rearrange("(a p) f -> p a f", p=128), writes=[unk])
                for a in range(4):
                    for q4 in range(4):
                        b = (a * 4 + q4) % 8
                        for j in range(4):
                            kc = q4 * 4 + j
                            S.op('pe', lambda e, kc=kc, j=j, b=b, a=a, un=un: e.transpose(
                                PS[b][:, j * 128:(j + 1) * 128], un[:, a, kc * 128:(kc + 1) * 128], ident[:]),
                                 [unk, 'ident'], [PK[b]])
                        evac(ut[:, q4 * 4:q4 * 4 + 4, a * 128:(a + 1) * 128],
                             PS[b][:, :].rearrange("p (n d) -> p n d", d=128), [PK[b]], [utk])
                S.dma('sp', WUT[eb], ut.rearrange("p a b -> p (a b)"), reads=[utk])
            S.barrier()

            AR.reset()
            WB = [AR.bf16([128, 16, 512])] * 2
            HT = AR.bf16([128, 16, 512]); QQ = AR.bf16([128, 16, 512])
            XT = [AR.f32([128, D])] * 2; XN = AR.f32([128, D])
            KN = AR.f32([128, 2, 128]); KT = AR.bf16([128, 2, 128])
            SC = [AR.f32([128, 8, 2, 128]) for _ in range(4)]
            TAU = [AR.f32([128, 8]) for _ in range(4)]; NBt = [AR.f32([128, 8]) for _ in range(4)]
            V16 = AR.f32([128, 8, 2, 16]); WK = AR.f32([128, 256]); CAND = AR.f32([128, 8, 256])
            SV = AR.f32([128, 8, 16]); ZS = AR.f32([128, 8]); NEGM = AR.f32([128, 8]); JK = AR.f32([128, 16])
            UTb = [WB[0], QQ]; UTk = ['WB0', 'QQ']
            VBb = [AR.bf16([128, 4, D]) for _ in range(2)]
            GE = [AR.f32([128, 512]) for _ in range(2)]
            TT = [AR.f32([128, 4, 512]) for _ in range(2)]
            EX = [AR.bf16([128, 4, 512]) for _ in range(2)]
            identb = AR.bf16([128, 128])
            S.op('dve', lambda e: e.tensor_copy(identb, ident[:]), ['ident'], ['identb'])
            GG = AR.f32([128, 512]); GAs = AR.f32([128, 512])
            GATt = [AR.bf16([128, 4, 128]) for _ in range(2)]
            G2B = AR.f32([128, D]); LG = AR.f32([128, D]); LB = AR.f32([128, D])
            S.dma('sp', LG, ln_g[l, 1:2, :].broadcast_to([128, D]), writes=['LG'])
            S.dma('sp', LB, ln_b[l, 1:2, :].broadcast_to([128, D]), writes=['LB'])
            S.dma('sp', KN, pkeys[l].rearrange("a n d -> n a d"), writes=['KN'])
            for a in range(2):
                S.op('pe', lambda e, a=a: e.transpose(PS[7][:, a * 128:(a + 1) * 128], KN[:, a, :], ident[:]),
                     ['KN', 'ident'], [PK[7]])
            S.op('dve', lambda e: e.tensor_copy(KT, PS[7][:, 0:256].rearrange("p (a n) -> p a n", a=2)), [PK[7]], ['KT'])
            wi = 0; ei = 0
            for g in range(9):
                cond = 0 if g < 8 else 1
                if g == 0 or g == 8:
                    S.dma('sp', G2B, MOD[cond:cond + 1, 5 * D:6 * D].broadcast_to([128, D]), writes=['GB'])
                for t in range(4):
                    tok0 = (g * 4 + t) * 128
                    ln_transpose(X1[tok0:tok0 + 128, :], XT[0], 'XT0', XN, 'XN', small, 'small',
                                 HT, 'HT', t * 128, cond, 2, 3, [4, 5, 6, 7])
                for b4 in range(4):
                    wb = WB[0]; wk = 'WB0'; wi += 1
                    S.dma('pool', wb, w_pq[l, :, b4 * 512:(b4 + 1) * 512].rearrange("(kc p) n -> p kc n", p=128),
                          writes=[wk])
                    for sub in range(4):
                        j = b4 * 4 + sub
                        b = 4 + (j % 4)
                        for kc in range(16):
                            S.op('pe', lambda e, kc=kc, wb=wb, sub=sub, b=b: e.matmul(
                                PS[b][:, :], wb[:, kc, sub * 128:(sub + 1) * 128], HT[:, kc, :],
                                start=(kc == 0), stop=(kc == 15)), [wk, 'HT'], [PK[b]])
                        evac(QQ[:, j, :], PS[b][:, :], [PK[b]], ['QQ'])
                for t in range(4):
                    sc = SC[t]; sck = 'SC%d' % t
                    for q4 in range(4):
                        b = 4 + q4
                        for jj in range(4):
                            j = q4 * 4 + jj
                            S.op('pe', lambda e, j=j, jj=jj, b=b, t=t: e.matmul(
                                PS[b][:, jj * 128:(jj + 1) * 128], QQ[:, j, t * 128:(t + 1) * 128], KT[:, j % 2, :],
                                start=True, stop=True), ['QQ', 'KT'], [PK[b]])
                        evac(sc.rearrange("p h a n -> p (h a n)")[:, q4 * 512:(q4 + 1) * 512], PS[b][:, :], [PK[b]], [sck])
                    for h in range(8):
                        for a in range(2):
                            S.op('dve', lambda e, h=h, a=a, sc=sc: e.max(V16[:, h, a, 0:8], sc[:, h, a, :]), [sck], ['V16'])
                            S.op('dve', lambda e, h=h, a=a, sc=sc: e.match_replace(WK[:, 0:128], V16[:, h, a, 0:8],
                                                                                  sc[:, h, a, :], -1e30),
                                 [sck, 'V16'], ['WK'])
                            S.op('dve', lambda e, h=h, a=a: e.max(V16[:, h, a, 8:16], WK[:, 0:128]), ['WK'], ['V16'])
                    C4 = CAND.rearrange("p h (i j) -> p h i j", i=16)
                    S.op('dve', lambda e, C4=C4: e.tensor_tensor(
                        C4, V16[:, :, 0, :].unsqueeze(3).broadcast_to([128, 8, 16, 16]),
                        V16[:, :, 1, :].unsqueeze(2).broadcast_to([128, 8, 16, 16]), ALU.add), ['V16'], ['CAND'])
                    for h in range(8):
                        S.op('dve', lambda e, h=h: e.max(SV[:, h, 0:8], CAND[:, h, :]), ['CAND'], ['SV'])
                        S.op('dve', lambda e, h=h: e.match_replace(WK[:, :], SV[:, h, 0:8], CAND[:, h, :], -1e30),
                             ['CAND', 'SV'], ['WK'])
                        S.op('dve', lambda e, h=h: e.max(SV[:, h, 8:16], WK[:, :]), ['WK'], ['SV'])
                    S.op('dve', lambda e, t=t: e.tensor_copy(TAU[t], SV[:, :, 15]), ['SV'], ['TAU%d' % t])
                    S.op('dve', lambda e: e.tensor_scalar(NEGM, SV[:, :, 0], -1.0, None, ALU.mult), ['SV'], ['NEGM'])
                    for h in range(8):
                        S.op('act', lambda e, h=h: e.activation(out=JK, in_=SV[:, h, :], func=AF.Exp,
                                                                bias=NEGM[:, h:h + 1], scale=1.0,
                                                                accum_out=ZS[:, h:h + 1]), ['SV', 'NEGM'], ['JK', 'ZS'])
                    S.op('act', lambda e: e.activation(out=ZS, in_=ZS, func=AF.Ln), ['ZS'], ['ZS'])
                    S.op('dve', lambda e, t=t: e.tensor_tensor(NBt[t], NEGM, ZS, ALU.subtract), ['NEGM', 'ZS'],
                         ['NB%d' % t])
                for t in range(4):
                    tok0 = (g * 4 + t) * 128
                    sc = SC[t]; sck = 'SC%d' % t
                    def stage_a(eb, t=t):
                        p2 = eb % 2
                        utb = UTb[p2]; utk = UTk[p2]; vbb = VBb[p2]; ge = GE[p2]; tt = TT[0]; ex = EX[0]; gat = GATt[p2]
                        S.dma('sp', utb.rearrange("p a b -> p (a b)"), WUT[eb], writes=[utk])
                        S.dma('act', vbb, WVB[eb * 512:(eb + 1) * 512, :].rearrange("(a p) f -> p a f", p=128),
                              writes=['VBb%d' % p2])
                        pa = 4 + p2
                        for kc in range(16):
                            S.op('pe', lambda e, kc=kc, utb=utb, pa=pa, t=t: e.matmul(
                                PS[pa][:, :], HT[:, kc, t * 128:(t + 1) * 128], utb[:, kc, :],
                                start=(kc == 0), stop=(kc == 15)), ['HT', utk], [PK[pa]])
                        S.op('act', lambda e, ge=ge, pa=pa: e.activation(out=ge, in_=PS[pa][:, :], func=AF.Square),
                             [PK[pa]], ['GE%d' % p2])
                        S.op('pool', lambda e, ge=ge: e.tensor_scalar(ge, ge, 0.044715 * GELU_C, GELU_C, ALU.mult,
                                                                      ALU.add), ['GE%d' % p2], ['GE%d' % p2])
                        S.op('dve', lambda e, ge=ge, pa=pa: e.tensor_tensor(ge, ge, PS[pa][:, :], ALU.mult),
                             ['GE%d' % p2, PK[pa]], ['GE%d' % p2])
                        S.op('act', lambda e, ge=ge: e.activation(out=ge, in_=ge, func=AF.Sigmoid), ['GE%d' % p2],
                             ['GE%d' % p2])
                    def stage_b(eb, t=t, sc=sc, sck=sck):
                        p2 = eb % 2; pa = 4 + p2
                        vbb = VBb[p2]; ge = GE[p2]; gat = GATt[p2]
                        for hf in range(2):
                            tth = TT[hf]; exh = EX[hf]; ttk = 'TT%d' % hf; exk = 'EX%d' % hf
                            t4 = tth.rearrange("p h (a n) -> p h a n", a=4)
                            S.op('pool', lambda e, t4=t4, sc=sc, eb=eb, hf=hf: e.tensor_tensor(
                                t4, sc[:, hf * 4:hf * 4 + 4, 0, eb * 4:eb * 4 + 4].unsqueeze(3).broadcast_to([128, 4, 4, 128]),
                                sc[:, hf * 4:hf * 4 + 4, 1, :].unsqueeze(2).broadcast_to([128, 4, 4, 128]), ALU.add),
                                 [sck], [ttk])
                            for hh in range(4):
                                h = hf * 4 + hh
                                S.op('act', lambda e, h=h, hh=hh, exh=exh, tth=tth, t=t: e.activation(
                                    out=exh[:, hh, :], in_=tth[:, hh, :], func=AF.Exp, bias=NBt[t][:, h:h + 1], scale=1.0),
                                     [ttk, 'NB%d' % t], [exk])
                            for hh in range(4):
                                h = hf * 4 + hh
                                S.op('dve', lambda e, h=h, hh=hh, exh=exh, tth=tth, t=t: e.scalar_tensor_tensor(
                                    out=exh[:, hh, :], in0=tth[:, hh, :], scalar=TAU[t][:, h:h + 1], in1=exh[:, hh, :],
                                    op0=ALU.is_ge, op1=ALU.mult), [ttk, 'TAU%d' % t, exk], [exk])
                            for hh in range(4):
                                h = hf * 4 + hh
                                S.op('pe', lambda e, h=h, hh=hh, exh=exh: e.matmul(
                                    PS[7][:, :], identb, exh[:, hh, :], start=(h == 0), stop=(h == 7)),
                                     [exk, 'identb'], [PK[7]])
                        S.op('dve', lambda e, ge=ge: e.tensor_tensor(GG, PS[7][:, :], ge, ALU.mult),
                             [PK[7], 'GE%d' % p2], ['GG'])
                        S.op('dve', lambda e, pa=pa: e.tensor_tensor(GAs, GG, PS[pa][:, :], ALU.mult),
                             ['GG', PK[pa]], ['GAs'])
                        pt = 6
                        for a in range(4):
                            S.op('pe', lambda e, a=a, pt=pt: e.transpose(PS[pt][:, a * 128:(a + 1) * 128],
                                                                         GAs[:, a * 128:(a + 1) * 128], ident[:]),
                                 ['GAs', 'ident'], [PK[pt]])
                        S.op('act', lambda e, gat=gat, pt=pt: e.activation(
                            out=gat, in_=PS[pt][:, :].rearrange("p (a n) -> p a n", a=4), func=AF.Copy),
                             [PK[pt]], ['GAT%d' % p2])
                        for a in range(4):
                            for cb in range(4):
                                S.op('pe', lambda e, a=a, cb=cb, gat=gat, vbb=vbb, eb=eb: e.matmul(
                                    PS[cb][:, :], gat[:, a, :], vbb[:, a, cb * 512:(cb + 1) * 512],
                                    start=(eb == 0 and a == 0), stop=(eb == 31 and a == 3)),
                                     ['GAT%d' % p2, 'VBb%d' % p2], [PK[cb]])
                    stage_a(0)
                    for eb in range(32):
                        if eb + 1 < 32:
                            stage_a(eb + 1)
                        stage_b(eb)
                    post_ln(l, 1, tok0, cond, [0, 1, 2, 3], X1[tok0:tok0 + 128, :], xdst(l, tok0, 128),
                            (XT[0], XN, G2B, LG, LB, small), ['XT0', 'XN'])
            S.barrier()
        S.barrier()
    return nc


_CACHE = {}


def kernel(**inp):
    stage = int(os.environ.get("MK_STAGE", "99"))
    if stage not in _CACHE:
        _CACHE[stage] = build(stage)
    nc = _CACHE[stage]
    f = lambda a: np.ascontiguousarray(np.asarray(a, dtype=np.float32))
    ident = np.eye(128, dtype=np.float32)
    s_, c_ = np.meshgrid(np.arange(64), np.arange(64), indexing='ij')
    mfw = (c_ >= s_).astype(np.float32)
    mbw = (c_ <= s_).astype(np.float32)
    rm = np.ones((128, 512), np.float32); rm[:, ::64] = 0.0
    shared = {k: f(inp[k]) for k in ("w_in", "w_conv", "w_a", "w_gk_up", "b_gk", "w_gla_norm", "w_b", "w_o", "w_ada",
                                      "b_ada", "ln_g", "ln_b", "w_pq", "peer_keys", "peer_u", "peer_v")}
    shared.update(c_ident=ident, c_mf=mfw, c_mb=mbw, c_rm=rm)
    x_prompt = f(inp["x_prompt"]); x_sample = f(inp["x_sample"]); sg = f(inp["state_gla"])
    c = f(inp["c"]); c_ctx = f(inp["c_ctx"])
    in_maps = []
    for i in range(8):
        m = dict(shared)
        m["xs"] = x_sample[i]
        m["xp"] = np.ascontiguousarray(x_prompt[2 * i:2 * i + 2].reshape(512, D))
        m["st0"] = np.ascontiguousarray(sg[i])
        m["cvec"] = np.ascontiguousarray(np.stack([c[i], c_ctx], 0))
        in_maps.append(m)
    ncore = int(os.environ.get("MK_NCORE", "8"))
    res = run_bass_kernel_spmd(nc, in_maps[:ncore], core_ids=list(range(ncore)))
    R = res.results
    if ncore < 8:
        return R
    y_sample = np.stack([R[i]["ys"] for i in range(8)], 0).astype(np.float32)
    y_prompt = np.concatenate([R[i]["yp"].reshape(2, 256, D) for i in range(8)], 0).astype(np.float32)
    new_state = np.concatenate([R[i]["ns"] for i in range(8)], 0).astype(np.float32)
    return (y_prompt, y_sample, new_state)
```

```python
import contextlib
import os
import numpy as np
import concourse.bass as bass
import concourse.mybir as mybir
from concourse.bass_utils import run_bass_kernel_spmd

F32 = mybir.dt.float32
BF16 = mybir.dt.bfloat16
AF = mybir.ActivationFunctionType
ALU = mybir.AluOpType
AX = mybir.AxisListType

D = 2048
DEPTH = 2
NTOK = 4608
TS = 4096
NT = NTOK // 128
DIN = 13344
ALPHA = (2.0 * DEPTH) ** 0.25
EPS = 1e-5
NSD = 8
GELU_C = 1.5957691216057308


class Sched:
    def __init__(self, nc, es):
        self.nc = nc
        self.eng = {'pe': nc.tensor, 'act': nc.scalar, 'dve': nc.vector, 'pool': nc.gpsimd, 'sp': nc.sync}
        self.sem = {e: es.enter_context(nc.semaphore('s_' + e)) for e in ('pe', 'act', 'dve', 'pool')}
        self.cnt = {e: 0 for e in self.sem}
        self.waited = {e: {} for e in self.eng}
        self.dsem = {q: [es.enter_context(nc.semaphore('d_%s%d' % (q, i))) for i in range(NSD)]
                     for q in ('sp', 'pool', 'act')}
        self.dcnt = {q: 0 for q in self.dsem}
        self.lastw = {}
        self.readers = {}

    def _wait(self, e, ev):
        key, val, semh = ev
        if e == 'pe' and key == ('c', 'pe'):
            return
        if self.waited[e].get(key, 0) < val:
            self.eng[e].wait_ge(semh, val)
            self.waited[e][key] = val

    def _deps(self, e, reads, writes):
        for k in reads:
            w = self.lastw.get(k)
            if w is not None:
                self._wait(e, w)
        for k in writes:
            w = self.lastw.get(k)
            if w is not None:
                self._wait(e, w)
            for r in self.readers.get(k, {}).values():
                self._wait(e, r)

    def _reg(self, ev, reads, writes):
        for k in reads:
            self.readers.setdefault(k, {})[ev[0]] = ev
        for k in writes:
            self.lastw[k] = ev
            self.readers[k] = {}

    def op(self, e, fn, reads=(), writes=()):
        self._deps(e, reads, writes)
        ins = fn(self.eng[e])
        self.cnt[e] += 1
        ins.then_inc(self.sem[e], 1)
        self._reg((('c', e), self.cnt[e], self.sem[e]), reads, writes)

    def dma(self, q, out, in_, reads=(), writes=(), slow=False):
        i = self.dcnt[q]
        self.dcnt[q] += 1
        slot = i % NSD
        val = 16 * (i // NSD + 1)
        semh = self.dsem[q][slot]
        key = ('d', q, slot)
        if val > 16:
            self._wait(q, (key, val - 16, semh))
        self._deps(q, reads, writes)
        if slow:
            ins = self.eng[q].dma_start(out=out, in_=in_, allow_slow_non_contiguous=True)
        else:
            ins = self.eng[q].dma_start(out=out, in_=in_)
        ins.then_inc(semh, 16)
        self._reg((key, val, semh), reads, writes)

    def barrier(self):
        evs = []
        for e in self.sem:
            if self.cnt[e] > 0:
                evs.append((('c', e), self.cnt[e], self.sem[e]))
        for q in self.dsem:
            n = self.dcnt[q]
            for slot in range(NSD):
                if n > slot:
                    k = (n - 1 - slot) // NSD + 1
                    evs.append((('d', q, slot), 16 * k, self.dsem[q][slot]))
        for e in self.eng:
            for ev in evs:
                key, val, semh = ev
                if self.waited[e].get(key, 0) < val:
                    self.eng[e].wait_ge(semh, val)
                    self.waited[e][key] = val


class Arena:
    def __init__(self, ar, nwords):
        self.ar = ar
        self.n = nwords
        self.off = 0

    def reset(self):
        self.off = 0

    def f32(self, shape):
        n = int(np.prod(shape[1:]))
        assert self.off + n <= self.n, ("arena overflow", self.off, n)
        v = self.ar[0:shape[0], self.off:self.off + n]
        self.off += n
        return _shape(v, shape)

    def bf16(self, shape):
        n = int(np.prod(shape[1:]))
        nw = (n + 1) // 2
        assert self.off + nw <= self.n, ("arena overflow", self.off, nw)
        v = self.ar[0:shape[0], self.off:self.off + nw].bitcast(BF16)[:, 0:n]
        self.off += nw
        return _shape(v, shape)


def _shape(v, shape):
    if len(shape) == 2:
        return v
    if len(shape) == 3:
        return v.rearrange("p (a b) -> p a b", a=shape[1])
    if len(shape) == 4:
        return v.rearrange("p (a b c) -> p a b c", a=shape[1], b=shape[2])
    raise ValueError(shape)


def build(stage=99):
    nc = bass.Bass("TRN2", target_bir_lowering=False)

    def din(name, shape, dt=F32):
        return nc.dram_tensor(name, list(shape), dt, kind="ExternalInput").ap()

    def dout(name, shape, dt=F32):
        return nc.dram_tensor(name, list(shape), dt, kind="ExternalOutput").ap()

    def dscr(name, shape, dt=F32):
        kind = "ExternalOutput" if os.environ.get("MK_DEBUG") else "Internal"
        return nc.dram_tensor(name, list(shape), dt, kind=kind).ap()

    xs = din("xs", [TS, D]); xp = din("xp", [512, D])
    st0 = din("st0", [DEPTH, 2, 4, 256, 512])
    cvec = din("cvec", [2, D])
    w_in = din("w_in", [DEPTH, D, DIN]); w_conv = din("w_conv", [DEPTH, 3, 1024])
    w_a = din("w_a", [DEPTH, 1024, D]); w_gk = din("w_gk_up", [DEPTH, 2, 16, 1024])
    b_gk = din("b_gk", [DEPTH, 2, 1024]); w_gn = din("w_gla_norm", [DEPTH, D])
    w_b = din("w_b", [DEPTH, D, D]); w_o = din("w_o", [DEPTH, D, D])
    w_ada = din("w_ada", [DEPTH, D, 6 * D]); b_ada = din("b_ada", [DEPTH, 6 * D])
    ln_g = din("ln_g", [DEPTH, 2, D]); ln_b = din("ln_b", [DEPTH, 2, D])
    w_pq = din("w_pq", [DEPTH, D, D]); pkeys = din("peer_keys", [DEPTH, 2, 128, 128])
    peer_u = din("peer_u", [DEPTH, 16384, D]); peer_v = din("peer_v", [DEPTH, 16384, D])
    c_ident = din("c_ident", [128, 128]); c_mf = din("c_mf", [64, 64]); c_mb = din("c_mb", [64, 64])
    c_rm = din("c_rm", [128, 512])

    ys = dout("ys", [TS, D]); yp = dout("yp", [512, D])
    ns = dout("ns", [2, DEPTH, 2, 4, 256, 512])

    ZT = dscr("ZT", [9248, NTOK]); Z = dscr("Z", [NTOK, 4096])
    VAT = dscr("VAT", [1024, NTOK], BF16)
    OF = dscr("OF", [NTOK, D]); OB = dscr("OB", [NTOK, D])
    X1 = dscr("X1", [NTOK, D]); X2 = dscr("X2", [NTOK, D])
    MOD = dscr("MOD", [2, 6 * D])
    WUT = dscr("WUT", [32, 128, 8192], BF16); WVB = dscr("WVB", [16384, D], BF16)

    es = contextlib.ExitStack()
    with es:
        S = Sched(nc, es)
        NW = 51900
        ARt = es.enter_context(nc.sbuf_tensor("arena", [128, NW], F32))
        AR = Arena(ARt, NW)
        ident = es.enter_context(nc.sbuf_tensor("ident", [128, 128], F32))
        mf = es.enter_context(nc.sbuf_tensor("mf", [64, 64], F32))
        mb = es.enter_context(nc.sbuf_tensor("mb", [64, 64], F32))
        rmk = es.enter_context(nc.sbuf_tensor("rmk", [128, 512], F32))
        modT = es.enter_context(nc.sbuf_tensor("modT", [128, 2, 4, 16], F32))
        small = es.enter_context(nc.sbuf_tensor("small", [128, 256], F32))
        PS = [es.enter_context(nc.psum_tensor("ps%d" % i, [128, 512], F32)) for i in range(8)]
        PK = ["ps%d" % i for i in range(8)]

        S.dma('sp', ident[:], c_ident[:, :], writes=['ident'])
        S.dma('sp', mf[:], c_mf[:, :], writes=['mf'])
        S.dma('sp', mb[:], c_mb[:, :], writes=['mb'])
        S.dma('sp', rmk[:], c_rm[:, :], writes=['rmk'])

        def xsrc(l, t0, n):
            if l == 0:
                return xs[t0:t0 + n, :] if t0 < TS else xp[t0 - TS:t0 - TS + n, :]
            return X2[t0:t0 + n, :]

        def xdst(l, t0, n):
            if l == DEPTH - 1:
                return ys[t0:t0 + n, :] if t0 < TS else yp[t0 - TS:t0 - TS + n, :]
            return X2[t0:t0 + n, :]

        rr = {'ev': 0}

        def evac(out, in_, reads, writes):
            rr['ev'] += 1
            if rr['ev'] % 2:
                S.op('act', lambda e: e.activation(out=out, in_=in_, func=AF.Copy), reads, writes)
            else:
                S.op('dve', lambda e: e.tensor_copy(out, in_), reads, writes)

        def ln_stats(xt, xk, sm, smk):
            for j in range(4):
                S.op('dve', lambda e, j=j: e.bn_stats(sm[:, j * 6:(j + 1) * 6], xt[:, j * 512:(j + 1) * 512]),
                     [xk], [smk])
            S.op('dve', lambda e: e.bn_aggr(sm[:, 24:26], sm[:, 0:24]), [smk], [smk])
            S.op('act', lambda e: e.activation(out=sm[:, 26:27], in_=sm[:, 25:26], func=AF.Sqrt, bias=EPS, scale=1.0),
                 [smk], [smk])
            S.op('dve', lambda e: e.reciprocal(sm[:, 27:28], sm[:, 26:27]), [smk], [smk])
            S.op('dve', lambda e: e.scalar_tensor_tensor(out=sm[:, 28:29], in0=sm[:, 24:25], scalar=-1.0,
                                                          in1=sm[:, 27:28], op0=ALU.mult, op1=ALU.mult),
                 [smk], [smk])
            return sm[:, 27:28], sm[:, 28:29]

        def ln_transpose(src_ap, xt, xk, xn, xnk, sm, smk, HT, hk, col0, cond, jsh, jsc, psb):
            S.dma('sp', xt, src_ap, writes=[xk])
            rstd, nb = ln_stats(xt, xk, sm, smk)
            S.op('act', lambda e: e.activation(out=xn, in_=xt, func=AF.Identity, bias=nb, scale=rstd),
                 [xk, smk], [xnk])
            for q4 in range(4):
                b = psb[q4 % len(psb)]
                for j in range(4):
                    kc = q4 * 4 + j
                    S.op('pe', lambda e, kc=kc, j=j, b=b: e.transpose(PS[b][:, j * 128:(j + 1) * 128],
                                                                      xn[:, kc * 128:(kc + 1) * 128], ident[:]),
                         [xnk, 'ident'], [PK[b]])
                for j in range(4):
                    kc = q4 * 4 + j
                    o_ = HT[:, kc, col0:col0 + 128]
                    i_ = PS[b][:, j * 128:(j + 1) * 128]
                    if cond is None:
                        evac(o_, i_, [PK[b]], [hk])
                    elif kc % 2:
                        S.op('dve', lambda e, o_=o_, i_=i_, kc=kc: e.tensor_scalar(
                            o_, i_, modT[:, cond, jsc, kc:kc + 1], modT[:, cond, jsh, kc:kc + 1], ALU.mult, ALU.add),
                             [PK[b], 'modT'], [hk])
                    else:
                        S.op('act', lambda e, o_=o_, i_=i_, kc=kc: e.activation(
                            out=o_, in_=i_, func=AF.Identity, bias=modT[:, cond, jsh, kc:kc + 1],
                            scale=modT[:, cond, jsc, kc:kc + 1]), [PK[b], 'modT'], [hk])

        def post_ln(l, which, t0, cond, acc_tiles, xin_ap, dst_ap, bufs, ek):
            xt, xn, GB_, LG, LB, sm = bufs
            S.dma('sp', xt, xin_ap, writes=['pl_x'] + ek)
            for cb in range(4):
                sl = slice(cb * 512, (cb + 1) * 512)
                S.op('dve', lambda e, cb=cb, sl=sl: e.tensor_tensor(xn[:, sl], PS[acc_tiles[cb]][:, :], GB_[:, sl],
                                                                     ALU.mult),
                     [PK[acc_tiles[cb]], 'GB'], ['pl_n%d' % cb] + ek)
                S.op('dve', lambda e, sl=sl: e.scalar_tensor_tensor(out=xn[:, sl], in0=xt[:, sl], scalar=ALPHA,
                                                                     in1=xn[:, sl], op0=ALU.mult, op1=ALU.add),
                     ['pl_x', 'pl_n%d' % cb], ['pl_n%d' % cb] + ek)
            allk = ['pl_n%d' % cb for cb in range(4)]
            S.op('pool', lambda e: e.tensor_copy(xt, xn), allk, ['pl_x'] + ek)
            rstd, nb = ln_stats(xt, 'pl_x', sm, 'small')
            S.op('act', lambda e: e.activation(out=xn, in_=xt, func=AF.Identity, bias=nb, scale=rstd),
                 ['pl_x', 'small'], allk + ek)
            S.op('dve', lambda e: e.tensor_tensor(xt, xn, LG, ALU.mult), allk + ['LG'], ['pl_x'] + ek)
            S.op('pool', lambda e: e.tensor_tensor(xn, xt, LB, ALU.add), ['pl_x', 'LB'], allk + ek)
            S.dma('sp', dst_ap, xn, reads=allk)

        for l in range(int(os.environ.get("MK_LAYERS", DEPTH))):
            AR.reset()
            CT = AR.f32([128, 2, 16]); BA = AR.f32([2, 6 * D]); MS = AR.f32([2, 6 * D])
            WF = [AR.f32([128, 16, 512]) for _ in range(2)]
            for c in range(2):
                S.dma('sp', CT[:, c, :], cvec[c].rearrange("(kc p) -> p kc", p=128), writes=['CT'], slow=True)
            S.dma('sp', BA, b_ada[l:l + 1, :].broadcast_to([2, 6 * D]), writes=['BA'])
            S.op('act', lambda e: e.activation(out=CT, in_=CT, func=AF.Silu), ['CT'], ['CT'])
            for cb in range(24):
                wf = WF[cb % 2]; wk = 'WF%d' % (cb % 2)
                S.dma('sp' if cb % 2 else 'act', wf,
                      w_ada[l, :, cb * 512:(cb + 1) * 512].rearrange("(kc p) n -> p kc n", p=128), writes=[wk])
                b = cb % 2
                for kc in range(16):
                    S.op('pe', lambda e, kc=kc, wf=wf, b=b: e.matmul(PS[b][0:2, :], CT[:, :, kc], wf[:, kc, :],
                                                                     start=(kc == 0), stop=(kc == 15)),
                         ['CT', wk], [PK[b]])
                S.op('dve', lambda e, cb=cb, b=b: e.tensor_tensor(MS[:, cb * 512:(cb + 1) * 512], PS[b][0:2, :],
                                                                  BA[:, cb * 512:(cb + 1) * 512], ALU.add),
                     [PK[b], 'BA'], ['MS'])
            S.dma('sp', MOD[:, :], MS, reads=['MS'])
            S.barrier()
            for c in range(2):
                for jj, j in enumerate((0, 1, 3, 4)):
                    S.dma('sp', modT[:, c, jj, :], MOD[c, j * D:(j + 1) * D].rearrange("(kc p) -> p kc", p=128),
                          writes=['modT'], slow=True)
            for c in range(2):
                for jj in (1, 3):
                    S.op('dve', lambda e, c=c, jj=jj: e.tensor_scalar_add(modT[:, c, jj, :], modT[:, c, jj, :], 1.0),
                         ['modT'], ['modT'])
            if stage <= 0:
                continue

            AR.reset()
            WB = [AR.bf16([128, 16, 512]) for _ in range(2)]
            HT = AR.bf16([128, 16, 512])
            XT = [AR.f32([128, D]) for _ in range(2)]
            XN = AR.f32([128, D])
            STG = [AR.f32([128, 512]) for _ in range(4)]
            fm_blocks = [(c0, 512, c0) for c0 in range(0, 5120, 512)] + [(9216, 32, 5120)] + \
                        [(9248 + i * 512, 512, 5152 + i * 512) for i in range(8)]
            tm_blocks = [(5120 + i * 512, 512, i * 512) for i in range(8)]
            wi = 0; si = 0; pi = 0
            for g in range(9):
                for t in range(4):
                    tile = g * 4 + t
                    cond = 0 if tile < 32 else 1
                    ln_transpose(xsrc(l, tile * 128, 128), XT[t % 2], 'XT%d' % (t % 2), XN, 'XN', small, 'small',
                                 HT, 'HT', t * 128, cond, 0, 1, [4, 5, 6, 7])
                for (c0, ncol, r0) in fm_blocks:
                    wb = WB[wi % 2]; wk = 'WB%d' % (wi % 2); wi += 1
                    S.dma('pool', wb[:, :, 0:ncol],
                          w_in[l, :, c0:c0 + ncol].rearrange("(kc p) n -> p kc n", p=128), writes=[wk])
                    for sub in range((ncol + 127) // 128):
                        m = min(128, ncol - sub * 128)
                        b = pi % 4; pi += 1
                        for kc in range(16):
                            S.op('pe', lambda e, kc=kc, wb=wb, b=b, m=m, sub=sub: e.matmul(
                                PS[b][0:m, :], wb[:, kc, sub * 128:sub * 128 + m], HT[:, kc, :],
                                start=(kc == 0), stop=(kc == 15)), [wk, 'HT'], [PK[b]])
                        stg = STG[si % 4]; sk = 'STG%d' % (si % 4); si += 1
                        evac(stg[0:m, :], PS[b][0:m, :], [PK[b]], [sk])
                        S.dma('sp', ZT[r0 + sub * 128:r0 + sub * 128 + m, g * 512:(g + 1) * 512], stg[0:m, :],
                              reads=[sk])
                for (c0, ncol, z0) in tm_blocks:
                    wb = WB[wi % 2]; wk = 'WB%d' % (wi % 2); wi += 1
                    S.dma('pool', wb, w_in[l, :, c0:c0 + 512].rearrange("(kc p) n -> p kc n", p=128), writes=[wk])
                    for t in range(4):
                        b = pi % 4; pi += 1
                        for kc in range(16):
                            S.op('pe', lambda e, kc=kc, wb=wb, b=b, t=t: e.matmul(
                                PS[b][:, :], HT[:, kc, t * 128:(t + 1) * 128], wb[:, kc, :],
                                start=(kc == 0), stop=(kc == 15)), [wk, 'HT'], [PK[b]])
                        stg = STG[si % 4]; sk = 'STG%d' % (si % 4); si += 1
                        evac(stg, PS[b][:, :], [PK[b]], [sk])
                        tok0 = (g * 4 + t) * 128
                        S.dma('sp', Z[tok0:tok0 + 128, z0:z0 + 512], stg, reads=[sk])
            S.barrier()
            if stage <= 1:
                continue

            AR.reset()
            WC = AR.f32([128, 3, 8])
            for j in range(3):
                S.dma('sp', WC[:, j, :], w_conv[l, j].rearrange("(b p) -> p b", p=128), writes=['WC'], slow=True)
            CC = AR.f32([128, 4096]); CX = AR.f32([128, 4096]); CB = AR.f32([128, 4096]); CO = AR.f32([128, 4096])
            VA = AR.bf16([128, 4096])
            for (t0, T, latent) in ((0, 4096, True), (4096, 512, False)):
                for blk in range(8):
                    r = blk * 128
                    S.dma('sp', CC[:, 0:T], ZT[1024 + r:1024 + r + 128, t0:t0 + T], writes=['CC'])
                    S.dma('act', CX[:, 0:T], ZT[2048 + r:2048 + r + 128, t0:t0 + T], writes=['CX'])
                    S.dma('sp', CB[:, 0:T], ZT[r:r + 128, t0:t0 + T], writes=['CB'])
                    S.op('pool', lambda e, T=T: e.tensor_tensor(CC[:, 0:T], CC[:, 0:T], CX[:, 0:T], ALU.mult),
                         ['CC', 'CX'], ['CC'])
                    S.op('dve', lambda e, T=T, blk=blk: e.tensor_scalar(CO[:, 0:T], CC[:, 0:T], WC[:, 1, blk:blk + 1], None,
                                                                         ALU.mult), ['CC', 'WC'], ['CO'])
                    if latent and blk < 4:
                        u3 = CC[:, 0:T].rearrange("p (r c) -> p r c", c=64)
                        o3 = CO[:, 0:T].rearrange("p (r c) -> p r c", c=64)
                        pairs = [(o3[:, :, 1:64], u3[:, :, 0:63], 0), (o3[:, :, 0:63], u3[:, :, 1:64], 2)]
                    elif latent:
                        pairs = [(CO[:, 64:T], CC[:, 0:T - 64], 0), (CO[:, 0:T - 64], CC[:, 64:T], 2)]
                    else:
                        u3 = CC[:, 0:T].rearrange("p (r c) -> p r c", c=256)
                        o3 = CO[:, 0:T].rearrange("p (r c) -> p r c", c=256)
                        pairs = [(o3[:, :, 1:256], u3[:, :, 0:255], 0), (o3[:, :, 0:255], u3[:, :, 1:256], 2)]
                    for (oo, uu, j) in pairs:
                        S.op('dve', lambda e, oo=oo, uu=uu, j=j, blk=blk: e.scalar_tensor_tensor(
                            out=oo, in0=uu, scalar=WC[:, j, blk:blk + 1], in1=oo, op0=ALU.mult, op1=ALU.add),
                             ['CC', 'CO', 'WC'], ['CO'])
                    S.op('pool', lambda e, T=T: e.tensor_tensor(VA[:, 0:T], CB[:, 0:T], CO[:, 0:T], ALU.mult),
                         ['CB', 'CO'], ['VA'])
                    S.dma('sp', VAT[r:r + 128, t0:t0 + T], VA[:, 0:T], reads=['VA'])
            S.barrier()
            if stage <= 2:
                continue

            for dr in range(2):
                AR.reset()
                OD = OF if dr == 0 else OB
                msk = mf if dr == 0 else mb
                mk = 'mf' if dr == 0 else 'mb'
                WGf = AR.f32([16, 1024]); WG = AR.bf16([16, 1024]); NBG = AR.f32([128, 8])
                S.dma('sp', WGf, w_gk[l, dr], writes=['WGf'])
                S.op('dve', lambda e: e.tensor_copy(WG, WGf), ['WGf'], ['WG'])
                S.dma('sp', NBG, b_gk[l, dr].rearrange("(j p) -> p j", p=128), writes=['NBG'], slow=True)
                S.op('dve', lambda e: e.tensor_scalar(NBG, NBG, -1.0, None, ALU.mult), ['NBG'], ['NBG'])
                LFf = AR.f32([16, 512]); LFb = AR.bf16([16, 512])
                Sst = [AR.f32([128, 2, 512]) for _ in range(4)]
                Sbf = [AR.bf16([128, 2, 512]) for _ in range(4)]
                QK = [AR.f32([128, 2, 2, 512]) for _ in range(2)]
                Gt = AR.f32([128, 512]); Bt = AR.f32([128, 512]); B2 = AR.f32([128, 512]); Et = AR.f32([128, 512])
                DEC = [AR.f32([128, 2, 8]) for _ in range(4)]
                QE = [AR.bf16([128, 2, 512]) for _ in range(4)]
                KE = [AR.bf16([128, 2, 512]) for _ in range(4)]
                KD32 = AR.f32([128, 512])
                KDT = [AR.bf16([64, 8, 256]) for _ in range(4)]
                Vt = [AR.bf16([64, 8, 512]) for _ in range(4)]
                AM = [AR.bf16([64, 64]) for _ in range(4)]
                OST = [AR.f32([64, 512]) for _ in range(4)]
                seqs = [(0, 4096, 512, None), (4096, 256, 256, 0), (4352, 256, 256, 1)]
                for (s0, T, BT, pbi) in seqs:
                    nch = BT // 64
                    for h in range(4):
                        if pbi is None:
                            S.dma('sp', Sst[h], st0[l, dr, h].rearrange("(kc p) e -> p kc e", p=128),
                                  writes=['S%d' % h])
                        else:
                            S.op('pool', lambda e, h=h: e.memset(Sst[h], 0.0), [], ['S%d' % h])
                        S.op('pool', lambda e, h=h: e.tensor_copy(Sbf[h], Sst[h]), ['S%d' % h], ['Sb%d' % h])
                    nblk = T // BT
                    order = range(nblk) if dr == 0 else range(nblk - 1, -1, -1)
                    for blk in order:
                        t0 = s0 + blk * BT
                        S.dma('sp', LFf[:, 0:BT], ZT[5120 + dr * 16:5136 + dr * 16, t0:t0 + BT], writes=['LFf'])
                        S.op('dve', lambda e, BT=BT: e.tensor_copy(LFb[:, 0:BT], LFf[:, 0:BT]), ['LFf'], ['LFb'])
                        for h in range(4):
                            qk = QK[h % 2]; qkk = 'QK%d' % (h % 2)
                            S.dma('sp', qk[:, 0, :, 0:BT],
                                  ZT[3072 + h * 256:3072 + (h + 1) * 256, t0:t0 + BT].rearrange("(kc p) t -> p kc t",
                                                                                                 p=128),
                                  writes=[qkk])
                            S.dma('act', qk[:, 1, :, 0:BT],
                                  ZT[4096 + h * 256:4096 + (h + 1) * 256, t0:t0 + BT].rearrange("(kc p) t -> p kc t",
                                                                                                 p=128),
                                  writes=[qkk])
                            S.dma('pool', Vt[h][:, 0:nch, :],
                                  Z[t0:t0 + BT, h * 512:(h + 1) * 512].rearrange("(n c) e -> c n e", c=64),
                                  writes=['V%d' % h])
                            for kc in range(2):
                                j = h * 2 + kc
                                b = 4 + (j % 2)
                                S.op('pe', lambda e, j=j, b=b, BT=BT: e.matmul(
                                    PS[b][:, 0:BT], WG[:, j * 128:(j + 1) * 128], LFb[:, 0:BT], start=True, stop=True),
                                     ['WG', 'LFb'], [PK[b]])
                                S.op('act', lambda e, j=j, b=b, BT=BT: e.activation(
                                    out=Et[:, 0:BT], in_=PS[b][:, 0:BT], func=AF.Exp, bias=NBG[:, j:j + 1], scale=-1.0),
                                     [PK[b], 'NBG'], ['Et'])
                                S.op('act', lambda e, BT=BT: e.activation(out=Et[:, 0:BT], in_=Et[:, 0:BT], func=AF.Ln,
                                                                          bias=1.0, scale=1.0), ['Et'], ['Et'])
                                S.op('dve', lambda e, BT=BT: e.tensor_scalar(Gt[:, 0:BT], Et[:, 0:BT], -1.0 / 16.0, -1.0,
                                                                             ALU.mult, ALU.max), ['Et'], ['Gt'])
                                S.op('dve', lambda e, BT=BT: e.tensor_tensor_scan(
                                    Bt[:, 0:BT], rmk[:, 0:BT], Gt[:, 0:BT], 0.0, ALU.mult, ALU.add),
                                     ['Gt', 'rmk'], ['Bt'])
                                B3 = Bt[:, 0:BT].rearrange("p (n c) -> p n c", c=64)
                                if dr == 1:
                                    S.op('dve', lambda e, BT=BT: e.tensor_tensor(B2[:, 0:BT], Gt[:, 0:BT], Bt[:, 0:BT],
                                                                                 ALU.subtract), ['Gt', 'Bt'], ['B2'])
                                    B23 = B2[:, 0:BT].rearrange("p (n c) -> p n c", c=64)
                                    S.op('dve', lambda e, B23=B23, B3=B3, nch=nch: e.tensor_tensor(
                                        B23, B23, B3[:, :, 63:64].broadcast_to([128, nch, 64]), ALU.add),
                                         ['B2', 'Bt'], ['B2'])
                                    Bu = B2; Buk = 'B2'; blast = B23[:, :, 0]
                                else:
                                    Bu = Bt; Buk = 'Bt'; blast = B3[:, :, 63]
                                dec = DEC[h][:, kc, 0:nch]
                                S.op('act', lambda e, dec=dec, blast=blast: e.activation(out=dec, in_=blast, func=AF.Exp),
                                     [Buk], ['DEC%d' % h])
                                S.op('act', lambda e, Bu=Bu, BT=BT: e.activation(out=Et[:, 0:BT], in_=Bu[:, 0:BT],
                                                                                  func=AF.Exp), [Buk], ['Et'])
                                S.op('dve', lambda e, h=h, kc=kc, qk=qk, BT=BT: e.scalar_tensor_tensor(
                                    out=QE[h][:, kc, 0:BT], in0=qk[:, 0, kc, 0:BT], scalar=1.0 / 16.0, in1=Et[:, 0:BT],
                                    op0=ALU.mult, op1=ALU.mult), [qkk, 'Et'], ['QE%d' % h])
                                S.op('act', lambda e, Bu=Bu, BT=BT: e.activation(out=Et[:, 0:BT], in_=Bu[:, 0:BT],
                                                                                  func=AF.Exp, scale=-1.0), [Buk], ['Et'])
                                S.op('dve', lambda e, h=h, kc=kc, qk=qk, BT=BT: e.tensor_tensor(
                                    qk[:, 1, kc, 0:BT], qk[:, 1, kc, 0:BT], Et[:, 0:BT], ALU.mult), [qkk, 'Et'], [qkk])
                                S.op('pool', lambda e, h=h, kc=kc, qk=qk, BT=BT: e.tensor_copy(
                                    KE[h][:, kc, 0:BT], qk[:, 1, kc, 0:BT]), [qkk], ['KE%d' % h])
                                K3 = qk[:, 1, kc, 0:BT].rearrange("p (n c) -> p n c", c=64)
                                D3 = KD32[:, 0:BT].rearrange("p (n c) -> p n c", c=64)
                                S.op('dve', lambda e, K3=K3, D3=D3, dec=dec, nch=nch: e.tensor_tensor(
                                    D3, K3, dec.unsqueeze(2).broadcast_to([128, nch, 64]), ALU.mult),
                                     [qkk, 'DEC%d' % h], ['KD32'])
                                for cq in range(0, nch, 4):
                                    bb = 6 + (cq // 4 + kc) % 2
                                    for c4 in range(4):
                                        ch = cq + c4
                                        S.op('pe', lambda e, ch=ch, c4=c4, bb=bb: e.transpose(
                                            PS[bb][0:64, c4 * 128:(c4 + 1) * 128], KD32[:, ch * 64:(ch + 1) * 64],
                                            ident[:]), ['KD32', 'ident'], [PK[bb]])
                                    evac(KDT[h][:, cq:cq + 4, kc * 128:(kc + 1) * 128],
                                         PS[bb][0:64, :].rearrange("p (n d) -> p n d", d=128), [PK[bb]], ['KDT%d' % h])
                        corder = range(nch) if dr == 0 else range(nch - 1, -1, -1)
                        for ch in corder:
                            cs = slice(ch * 64, (ch + 1) * 64)
                            for h in range(4):
                                S.op('pe', lambda e, h=h, cs=cs: e.matmul(PS[0][0:64, h * 64:(h + 1) * 64],
                                                                          KE[h][:, 0, cs], QE[h][:, 0, cs],
                                                                          start=True, stop=False),
                                     ['KE%d' % h, 'QE%d' % h], ['psA%d' % h])
                                S.op('pe', lambda e, h=h, cs=cs: e.matmul(PS[0][0:64, h * 64:(h + 1) * 64],
                                                                          KE[h][:, 1, cs], QE[h][:, 1, cs],
                                                                          start=False, stop=True),
                                     ['KE%d' % h, 'QE%d' % h], ['psA%d' % h])
                                S.op('dve', lambda e, h=h: e.tensor_tensor(AM[h], PS[0][0:64, h * 64:(h + 1) * 64], msk[:],
                                                                           ALU.mult), ['psA%d' % h, mk], ['AM%d' % h])
                                bo = 1 + (h % 2)
                                S.op('pe', lambda e, h=h, cs=cs, bo=bo: e.matmul(PS[bo][0:64, :], QE[h][:, 0, cs],
                                                                                 Sbf[h][:, 0, :], start=True, stop=False),
                                     ['QE%d' % h, 'Sb%d' % h], [PK[bo]])
                                S.op('pe', lambda e, h=h, cs=cs, bo=bo: e.matmul(PS[bo][0:64, :], QE[h][:, 1, cs],
                                                                                 Sbf[h][:, 1, :], start=False, stop=False),
                                     ['QE%d' % h, 'Sb%d' % h], [PK[bo]])
                                S.op('pe', lambda e, h=h, ch=ch, bo=bo: e.matmul(PS[bo][0:64, :], AM[h][:, :],
                                                                                 Vt[h][:, ch, :], start=False, stop=True),
                                     ['AM%d' % h, 'V%d' % h], [PK[bo]])
                                S.op('act', lambda e, h=h, bo=bo: e.activation(out=OST[h], in_=PS[bo][0:64, :],
                                                                               func=AF.Copy), [PK[bo]], ['OST%d' % h])
                                tk = t0 + ch * 64
                                S.dma('sp', OD[tk:tk + 64, h * 512:(h + 1) * 512], OST[h], reads=['OST%d' % h])
                                for kc in range(2):
                                    bs = 3 + ((h * 2 + kc) % 4) if False else (3 + (h * 2 + kc) % 2)
                                    S.op('pe', lambda e, h=h, ch=ch, kc=kc, bs=bs: e.matmul(
                                        PS[bs][:, :], KDT[h][:, ch, kc * 128:(kc + 1) * 128], Vt[h][:, ch, :],
                                        start=True, stop=True), ['KDT%d' % h, 'V%d' % h], [PK[bs]])
                                    S.op('dve', lambda e, h=h, ch=ch, kc=kc, bs=bs: e.scalar_tensor_tensor(
                                        out=Sst[h][:, kc, :], in0=Sst[h][:, kc, :], scalar=DEC[h][:, kc, ch:ch + 1],
                                        in1=PS[bs][:, :], op0=ALU.mult, op1=ALU.add),
                                         [PK[bs], 'DEC%d' % h, 'S%d' % h], ['S%d' % h])
                                S.op('pool', lambda e, h=h: e.tensor_copy(Sbf[h], Sst[h]), ['S%d' % h], ['Sb%d' % h])
                    if pbi is not None:
                        for h in range(4):
                            S.dma('sp', ns[pbi, l, dr, h].rearrange("(kc p) e -> p kc e", p=128), Sst[h],
                                  reads=['S%d' % h])
                S.barrier()
            if stage <= 3:
                continue

            AR.reset()
            WB = [AR.bf16([128, 16, 512]) for _ in range(2)]
            WA = [AR.bf16([128, 8, 512]) for _ in range(2)]
            HT = AR.bf16([128, 16, 512]); YT = AR.bf16([128, 16, 512]); VAs = AR.bf16([128, 8, 512])
            XT = AR.f32([128, D]); XN = AR.f32([128, D]); XO = AR.f32([128, D])
            WNB = AR.f32([128, D]); G1B = AR.f32([128, D]); LG = AR.f32([128, D]); LB = AR.f32([128, D])
            GAt = [AR.f32([128, 512]) for _ in range(2)]; GBt = [AR.f32([128, 512]) for _ in range(2)]
            S.dma('sp', WNB, w_gn[l:l + 1, :].broadcast_to([128, D]), writes=['WNB'])
            S.dma('sp', LG, ln_g[l, 0:1, :].broadcast_to([128, D]), writes=['LG'])
            S.dma('sp', LB, ln_b[l, 0:1, :].broadcast_to([128, D]), writes=['LB'])
            wi = 0; gi = 0
            for g in range(9):
                cond = 0 if g < 8 else 1
                if g == 0 or g == 8:
                    S.dma('sp', G1B, MOD[cond:cond + 1, 2 * D:3 * D].broadcast_to([128, D]), writes=['GB'])
                for t in range(4):
                    tok0 = (g * 4 + t) * 128
                    S.dma('sp', XT, OF[tok0:tok0 + 128, :], writes=['XT'])
                    S.dma('act', XN, OB[tok0:tok0 + 128, :], writes=['XN'])
                    S.dma('sp', XO, Z[tok0:tok0 + 128, 2048:4096], writes=['XO'])
                    S.op('pool', lambda e: e.tensor_tensor(XT, XT, XN, ALU.add), ['XT', 'XN'], ['XT'])
                    for h in range(4):
                        S.op('act', lambda e, h=h: e.activation(out=XN[:, h * 512:(h + 1) * 512],
                                                                in_=XT[:, h * 512:(h + 1) * 512], func=AF.Square,
                                                                accum_out=small[:, 32 + h:33 + h]),
                             ['XT'], ['XN', 'small'])
                    S.op('act', lambda e: e.activation(out=small[:, 36:40], in_=small[:, 32:36], func=AF.Sqrt, bias=EPS,
                                                       scale=1.0 / 512.0), ['small'], ['small'])
                    S.op('dve', lambda e: e.reciprocal(small[:, 40:44], small[:, 36:40]), ['small'], ['small'])
                    S.op('act', lambda e: e.activation(out=XO, in_=XO, func=AF.Silu), ['XO'], ['XO'])
                    for h in range(4):
                        hs = slice(h * 512, (h + 1) * 512)
                        S.op('dve', lambda e, h=h, hs=hs: e.scalar_tensor_tensor(
                            out=XN[:, hs], in0=XT[:, hs], scalar=small[:, 40 + h:41 + h], in1=WNB[:, hs],
                            op0=ALU.mult, op1=ALU.mult), ['XT', 'small', 'WNB'], ['XN'])
                    S.op('pool', lambda e: e.tensor_tensor(XN, XN, XO, ALU.mult), ['XN', 'XO'], ['XN'])
                    for q4 in range(4):
                        b = 4 + q4
                        for j in range(4):
                            kc = q4 * 4 + j
                            S.op('pe', lambda e, kc=kc, j=j, b=b: e.transpose(PS[b][:, j * 128:(j + 1) * 128],
                                                                              XN[:, kc * 128:(kc + 1) * 128], ident[:]),
                                 ['XN', 'ident'], [PK[b]])
                        evac(HT[:, q4 * 4:q4 * 4 + 4, t * 128:(t + 1) * 128],
                             PS[b][:, :].rearrange("p (n d) -> p n d", d=128), [PK[b]], ['HT'])
                S.dma('sp', VAs, VAT[:, g * 512:(g + 1) * 512].rearrange("(kc p) t -> p kc t", p=128), writes=['VAs'])
                for b4 in range(4):
                    wb = WB[wi % 2]; wk = 'WB%d' % (wi % 2)
                    wa = WA[wi % 2]; wak = 'WA%d' % (wi % 2); wi += 1
                    S.dma('pool', wb, w_b[l, :, b4 * 512:(b4 + 1) * 512].rearrange("(kc p) n -> p kc n", p=128),
                          writes=[wk])
                    S.dma('pool', wa, w_a[l, :, b4 * 512:(b4 + 1) * 512].rearrange("(kc p) n -> p kc n", p=128),
                          writes=[wak])
                    for sub in range(4):
                        fo = b4 * 4 + sub
                        ga = GAt[gi % 2]; gb = GBt[gi % 2]; gak = 'GA%d' % (gi % 2); gbk = 'GB%d_' % (gi % 2); gi += 1
                        S.dma('sp', ga, ZT[5152 + fo * 128:5152 + (fo + 1) * 128, g * 512:(g + 1) * 512], writes=[gak])
                        S.dma('act', gb, ZT[7200 + fo * 128:7200 + (fo + 1) * 128, g * 512:(g + 1) * 512], writes=[gbk])
                        S.op('act', lambda e, ga=ga: e.activation(out=ga, in_=ga, func=AF.Sigmoid), [gak], [gak])
                        S.op('act', lambda e, gb=gb: e.activation(out=gb, in_=gb, func=AF.Sigmoid), [gbk], [gbk])
                        pa = 0 + (fo % 2); pb = 2 + (fo % 2)
                        for kc in range(8):
                            S.op('pe', lambda e, kc=kc, wa=wa, sub=sub, pa=pa: e.matmul(
                                PS[pa][:, :], wa[:, kc, sub * 128:(sub + 1) * 128], VAs[:, kc, :],
                                start=(kc == 0), stop=(kc == 7)), [wak, 'VAs'], [PK[pa]])
                        for kc in range(16):
                            S.op('pe', lambda e, kc=kc, wb=wb, sub=sub, pb=pb: e.matmul(
                                PS[pb][:, :], wb[:, kc, sub * 128:(sub + 1) * 128], HT[:, kc, :],
                                start=(kc == 0), stop=(kc == 15)), [wk, 'HT'], [PK[pb]])
                        S.op('dve', lambda e, ga=ga, pa=pa: e.tensor_tensor(ga, ga, PS[pa][:, :], ALU.mult),
                             [gak, PK[pa]], [gak])
                        S.op('dve', lambda e, gb=gb, pb=pb: e.tensor_tensor(gb, gb, PS[pb][:, :], ALU.mult),
                             [gbk, PK[pb]], [gbk])
                        S.op('pool', lambda e, ga=ga, gb=gb, fo=fo: e.tensor_tensor(YT[:, fo, :], ga, gb, ALU.add),
                             [gak, gbk], ['YT'])
                for t in range(4):
                    tok0 = (g * 4 + t) * 128
                    for cb in range(4):
                        wb = WB[wi % 2]; wk = 'WB%d' % (wi % 2); wi += 1
                        S.dma('pool', wb, w_o[l, :, cb * 512:(cb + 1) * 512].rearrange("(kc p) n -> p kc n", p=128),
                              writes=[wk])
                        for kc in range(16):
                            S.op('pe', lambda e, kc=kc, wb=wb, cb=cb, t=t: e.matmul(
                                PS[4 + cb][:, :], YT[:, kc, t * 128:(t + 1) * 128], wb[:, kc, :],
                                start=(kc == 0), stop=(kc == 15)), [wk, 'YT'], [PK[4 + cb]])
                    post_ln(l, 0, tok0, cond, [4, 5, 6, 7], xsrc(l, tok0, 128), X1[tok0:tok0 + 128, :],
                            (XT, XN, G1B, LG, LB, small), ['XT', 'XN'])
            S.barrier()
            if stage <= 4:
                continue

            AR.reset()
            UN = [AR.f32([128, 4, D]) for _ in range(2)]
            UT = [AR.bf16([128, 16, 512]) for _ in range(2)]
            for i in range(8):
                S.dma('pool', WVB[i * 2048:(i + 1) * 2048, :], peer_v[l, i * 2048:(i + 1) * 2048, :])
            for eb in range(32):
                un = UN[eb % 2]; unk = 'UN%d' % (eb % 2); ut = UT[eb % 2]; utk = 'UT%d' % (eb % 2)
                S.dma('sp' if eb % 2 else 'act', un,
                      peer_u[l, eb * 512:(eb + 1) * 512, :].rearrange("(a p) f -> p a f", p=128), writes=[unk])
                for a in range(4):
                    for q4 in range(4):
                        b = (a * 4 + q4) % 8
                        for j in range(4):
                            kc = q4 * 4 + j
                            S.op('pe', lambda e, kc=kc, j=j, b=b, a=a, un=un: e.transpose(
                                PS[b][:, j * 128:(j + 1) * 128], un[:, a, kc * 128:(kc + 1) * 128], ident[:]),
                                 [unk, 'ident'], [PK[b]])
                        evac(ut[:, q4 * 4:q4 * 4 + 4, a * 128:(a + 1) * 128],
                             PS[b][:, :].rearrange("p (n d) -> p n d", d=128), [PK[b]], [utk])
                S.dma('sp', WUT[eb], ut.rearrange("p a b -> p (a b)"), reads=[utk])
            S.barrier()

            AR.reset()
            WB = [AR.bf16([128, 16, 512])] * 2
            HT = AR.bf16([128, 16, 512]); QQ = AR.bf16([128, 16, 512])
            XT = [AR.f32([128, D])] * 2; XN = AR.f32([128, D])
            KN = AR.f32([128, 2, 128]); KT = AR.bf16([128, 2, 128])
            SC = [AR.f32([128, 8, 2, 128]) for _ in range(4)]
            TAU = [AR.f32([128, 8]) for _ in range(4)]; NBt = [AR.f32([128, 8]) for _ in range(4)]
            V16 = AR.f32([128, 8, 2, 16]); WK = AR.f32([128, 256]); CAND = AR.f32([128, 8, 256])
            SV = AR.f32([128, 8, 16]); ZS = AR.f32([128, 8]); NEGM = AR.f32([128, 8]); JK = AR.f32([128, 16])
            UTb = [WB[0], QQ]; UTk = ['WB0', 'QQ']
            VBb = [AR.bf16([128, 4, D]) for _ in range(2)]
            GE = [AR.f32([128, 512]) for _ in range(2)]
            TT = [AR.f32([128, 4, 512]) for _ in range(2)]
            EX = [AR.bf16([128, 4, 512]) for _ in range(2)]
            identb = AR.bf16([128, 128])
            S.op('dve', lambda e: e.tensor_copy(identb, ident[:]), ['ident'], ['identb'])
            GG = AR.f32([128, 512]); GAs = AR.f32([128, 512])
            GATt = [AR.bf16([128, 4, 128]) for _ in range(2)]
            G2B = AR.f32([128, D]); LG = AR.f32([128, D]); LB = AR.f32([128, D])
            S.dma('sp', LG, ln_g[l, 1:2, :].broadcast_to([128, D]), writes=['LG'])
            S.dma('sp', LB, ln_b[l, 1:2, :].broadcast_to([128, D]), writes=['LB'])
            S.dma('sp', KN, pkeys[l].rearrange("a n d -> n a d"), writes=['KN'])
            for a in range(2):
                S.op('pe', lambda e, a=a: e.transpose(PS[7][:, a * 128:(a + 1) * 128], KN[:, a, :], ident[:]),
                     ['KN', 'ident'], [PK[7]])
            S.op('dve', lambda e: e.tensor_copy(KT, PS[7][:, 0:256].rearrange("p (a n) -> p a n", a=2)), [PK[7]], ['KT'])
            wi = 0; ei = 0
            for g in range(9):
                cond = 0 if g < 8 else 1
                if g == 0 or g == 8:
                    S.dma('sp', G2B, MOD[cond:cond + 1, 5 * D:6 * D].broadcast_to([128, D]), writes=['GB'])
                for t in range(4):
                    tok0 = (g * 4 + t) * 128
                    ln_transpose(X1[tok0:tok0 + 128, :], XT[0], 'XT0', XN, 'XN', small, 'small',
                                 HT, 'HT', t * 128, cond, 2, 3, [4, 5, 6, 7])
                for b4 in range(4):
                    wb = WB[0]; wk = 'WB0'; wi += 1
                    S.dma('pool', wb, w_pq[l, :, b4 * 512:(b4 + 1) * 512].rearrange("(kc p) n -> p kc n", p=128),
                          writes=[wk])
                    for sub in range(4):
                        j = b4 * 4 + sub
                        b = 4 + (j % 4)
                        for kc in range(16):
                            S.op('pe', lambda e, kc=kc, wb=wb, sub=sub, b=b: e.matmul(
                                PS[b][:, :], wb[:, kc, sub * 128:(sub + 1) * 128], HT[:, kc, :],
                                start=(kc == 0), stop=(kc == 15)), [wk, 'HT'], [PK[b]])
                        evac(QQ[:, j, :], PS[b][:, :], [PK[b]], ['QQ'])
                for t in range(4):
                    sc = SC[t]; sck = 'SC%d' % t
                    for q4 in range(4):
                        b = 4 + q4
                        for jj in range(4):
                            j = q4 * 4 + jj
                            S.op('pe', lambda e, j=j, jj=jj, b=b, t=t: e.matmul(
                                PS[b][:, jj * 128:(jj + 1) * 128], QQ[:, j, t * 128:(t + 1) * 128], KT[:, j % 2, :],
                                start=True, stop=True), ['QQ', 'KT'], [PK[b]])
                        evac(sc.rearrange("p h a n -> p (h a n)")[:, q4 * 512:(q4 + 1) * 512], PS[b][:, :], [PK[b]], [sck])
                    for h in range(8):
                        for a in range(2):
                            S.op('dve', lambda e, h=h, a=a, sc=sc: e.max(V16[:, h, a, 0:8], sc[:, h, a, :]), [sck], ['V16'])
                            S.op('dve', lambda e, h=h, a=a, sc=sc: e.match_replace(WK[:, 0:128], V16[:, h, a, 0:8],
                                                                                  sc[:, h, a, :], -1e30),
                                 [sck, 'V16'], ['WK'])
                            S.op('dve', lambda e, h=h, a=a: e.max(V16[:, h, a, 8:16], WK[:, 0:128]), ['WK'], ['V16'])
                    C4 = CAND.rearrange("p h (i j) -> p h i j", i=16)
                    S.op('dve', lambda e, C4=C4: e.tensor_tensor(
                        C4, V16[:, :, 0, :].unsqueeze(3).broadcast_to([128, 8, 16, 16]),
                        V16[:, :, 1, :].unsqueeze(2).broadcast_to([128, 8, 16, 16]), ALU.add), ['V16'], ['CAND'])
                    for h in range(8):
                        S.op('dve', lambda e, h=h: e.max(SV[:, h, 0:8], CAND[:, h, :]), ['CAND'], ['SV'])
                        S.op('dve', lambda e, h=h: e.match_replace(WK[:, :], SV[:, h, 0:8], CAND[:, h, :], -1e30),
                             ['CAND', 'SV'], ['WK'])
                        S.op('dve', lambda e, h=h: e.max(SV[:, h, 8:16], WK[:, :]), ['WK'], ['SV'])
                    S.op('dve', lambda e, t=t: e.tensor_copy(TAU[t], SV[:, :, 15]), ['SV'], ['TAU%d' % t])
                    S.op('dve', lambda e: e.tensor_scalar(NEGM, SV[:, :, 0], -1.0, None, ALU.mult), ['SV'], ['NEGM'])
                    for h in range(8):
                        S.op('act', lambda e, h=h: e.activation(out=JK, in_=SV[:, h, :], func=AF.Exp,
                                                                bias=NEGM[:, h:h + 1], scale=1.0,
                                                                accum_out=ZS[:, h:h + 1]), ['SV', 'NEGM'], ['JK', 'ZS'])
                    S.op('act', lambda e: e.activation(out=ZS, in_=ZS, func=AF.Ln), ['ZS'], ['ZS'])
                    S.op('dve', lambda e, t=t: e.tensor_tensor(NBt[t], NEGM, ZS, ALU.subtract), ['NEGM', 'ZS'],
                         ['NB%d' % t])
                for t in range(4):
                    tok0 = (g * 4 + t) * 128
                    sc = SC[t]; sck = 'SC%d' % t
                    def stage_a(eb, t=t):
                        p2 = eb % 2
                        utb = UTb[p2]; utk = UTk[p2]; vbb = VBb[p2]; ge = GE[p2]; tt = TT[0]; ex = EX[0]; gat = GATt[p2]
                        S.dma('sp', utb.rearrange("p a b -> p (a b)"), WUT[eb], writes=[utk])
                        S.dma('act', vbb, WVB[eb * 512:(eb + 1) * 512, :].rearrange("(a p) f -> p a f", p=128),
                              writes=['VBb%d' % p2])
                        pa = 4 + p2
                        for kc in range(16):
                            S.op('pe', lambda e, kc=kc, utb=utb, pa=pa, t=t: e.matmul(
                                PS[pa][:, :], HT[:, kc, t * 128:(t + 1) * 128], utb[:, kc, :],
                                start=(kc == 0), stop=(kc == 15)), ['HT', utk], [PK[pa]])
                        S.op('act', lambda e, ge=ge, pa=pa: e.activation(out=ge, in_=PS[pa][:, :], func=AF.Square),
                             [PK[pa]], ['GE%d' % p2])
                        S.op('pool', lambda e, ge=ge: e.tensor_scalar(ge, ge, 0.044715 * GELU_C, GELU_C, ALU.mult,
                                                                      ALU.add), ['GE%d' % p2], ['GE%d' % p2])
                        S.op('dve', lambda e, ge=ge, pa=pa: e.tensor_tensor(ge, ge, PS[pa][:, :], ALU.mult),
                             ['GE%d' % p2, PK[pa]], ['GE%d' % p2])
                        S.op('act', lambda e, ge=ge: e.activation(out=ge, in_=ge, func=AF.Sigmoid), ['GE%d' % p2],
                             ['GE%d' % p2])
                    def stage_b(eb, t=t, sc=sc, sck=sck):
                        p2 = eb % 2; pa = 4 + p2
                        vbb = VBb[p2]; ge = GE[p2]; gat = GATt[p2]
                        for hf in range(2):
                            tth = TT[hf]; exh = EX[hf]; ttk = 'TT%d' % hf; exk = 'EX%d' % hf
                            t4 = tth.rearrange("p h (a n) -> p h a n", a=4)
                            S.op('pool', lambda e, t4=t4, sc=sc, eb=eb, hf=hf: e.tensor_tensor(
                                t4, sc[:, hf * 4:hf * 4 + 4, 0, eb * 4:eb * 4 + 4].unsqueeze(3).broadcast_to([128, 4, 4, 128]),
                                sc[:, hf * 4:hf * 4 + 4, 1, :].unsqueeze(2).broadcast_to([128, 4, 4, 128]), ALU.add),
                                 [sck], [ttk])
                            for hh in range(4):
                                h = hf * 4 + hh
                                S.op('act', lambda e, h=h, hh=hh, exh=exh, tth=tth, t=t: e.activation(
                                    out=exh[:, hh, :], in_=tth[:, hh, :], func=AF.Exp, bias=NBt[t][:, h:h + 1], scale=1.0),
                                     [ttk, 'NB%d' % t], [exk])
                            for hh in range(4):
                                h = hf * 4 + hh
                                S.op('dve', lambda e, h=h, hh=hh, exh=exh, tth=tth, t=t: e.scalar_tensor_tensor(
                                    out=exh[:, hh, :], in0=tth[:, hh, :], scalar=TAU[t][:, h:h + 1], in1=exh[:, hh, :],
                                    op0=ALU.is_ge, op1=ALU.mult), [ttk, 'TAU%d' % t, exk], [exk])
                            for hh in range(4):
                                h = hf * 4 + hh
                                S.op('pe', lambda e, h=h, hh=hh, exh=exh: e.matmul(
                                    PS[7][:, :], identb, exh[:, hh, :], start=(h == 0), stop=(h == 7)),
                                     [exk, 'identb'], [PK[7]])
                    def stage_b2(eb, t=t, sc=sc, sck=sck):
                        p2 = eb % 2; pa = 4 + p2
                        vbb = VBb[p2]; ge = GE[p2]; gat = GATt[p2]
                        S.op('dve', lambda e, ge=ge: e.tensor_tensor(GG, PS[7][:, :], ge, ALU.mult),
                             [PK[7], 'GE%d' % p2], ['GG'])
                        S.op('dve', lambda e, pa=pa: e.tensor_tensor(GAs, GG, PS[pa][:, :], ALU.mult),
                             ['GG', PK[pa]], ['GAs'])
                        pt = 6
                        for a in range(4):
                            S.op('pe', lambda e, a=a, pt=pt: e.transpose(PS[pt][:, a * 128:(a + 1) * 128],
                                                                         GAs[:, a * 128:(a + 1) * 128], ident[:]),
                                 ['GAs', 'ident'], [PK[pt]])
                        S.op('act', lambda e, gat=gat, pt=pt: e.activation(
                            out=gat, in_=PS[pt][:, :].rearrange("p (a n) -> p a n", a=4), func=AF.Copy),
                             [PK[pt]], ['GAT%d' % p2])
                        for a in range(4):
                            for cb in range(4):
                                S.op('pe', lambda e, a=a, cb=cb, gat=gat, vbb=vbb, eb=eb: e.matmul(
                                    PS[cb][:, :], gat[:, a, :], vbb[:, a, cb * 512:(cb + 1) * 512],
                                    start=(eb == 0 and a == 0), stop=(eb == 31 and a == 3)),
                                     ['GAT%d' % p2, 'VBb%d' % p2], [PK[cb]])
                    stage_a(0)
                    for eb in range(32):
                        stage_b(eb)
                        if eb + 1 < 32:
                            stage_a(eb + 1)
                        stage_b2(eb)
                    post_ln(l, 1, tok0, cond, [0, 1, 2, 3], X1[tok0:tok0 + 128, :], xdst(l, tok0, 128),
                            (XT[0], XN, G2B, LG, LB, small), ['XT0', 'XN'])
            S.barrier()
        S.barrier()
    return nc


_CACHE = {}


def kernel(**inp):
    stage = int(os.environ.get("MK_STAGE", "99"))
    if stage not in _CACHE:
        _CACHE[stage] = build(stage)
    nc = _CACHE[stage]
    f = lambda a: np.ascontiguousarray(np.asarray(a, dtype=np.float32))
    ident = np.eye(128, dtype=np.float32)
    s_, c_ = np.meshgrid(np.arange(64), np.arange(64), indexing='ij')
    mfw = (c_ >= s_).astype(np.float32)
    mbw = (c_ <= s_).astype(np.float32)
    rm = np.ones((128, 512), np.float32); rm[:, ::64] = 0.0
    shared = {k: f(inp[k]) for k in ("w_in", "w_conv", "w_a", "w_gk_up", "b_gk", "w_gla_norm", "w_b", "w_o", "w_ada",
                                      "b_ada", "ln_g", "ln_b", "w_pq", "peer_keys", "peer_u", "peer_v")}
    shared.update(c_ident=ident, c_mf=mfw, c_mb=mbw, c_rm=rm)
    x_prompt = f(inp["x_prompt"]); x_sample = f(inp["x_sample"]); sg = f(inp["state_gla"])
    c = f(inp["c"]); c_ctx = f(inp["c_ctx"])
    in_maps = []
    for i in range(8):
        m = dict(shared)
        m["xs"] = x_sample[i]
        m["xp"] = np.ascontiguousarray(x_prompt[2 * i:2 * i + 2].reshape(512, D))
        m["st0"] = np.ascontiguousarray(sg[i])
        m["cvec"] = np.ascontiguousarray(np.stack([c[i], c_ctx], 0))
        in_maps.append(m)
    ncore = int(os.environ.get("MK_NCORE", "8"))
    res = run_bass_kernel_spmd(nc, in_maps[:ncore], core_ids=list(range(ncore)))
    R = res.results
    if ncore < 8:
        return R
    y_sample = np.stack([R[i]["ys"] for i in range(8)], 0).astype(np.float32)
    y_prompt = np.concatenate([R[i]["yp"].reshape(2, 256, D) for i in range(8)], 0).astype(np.float32)
    new_state = np.concatenate([R[i]["ns"] for i in range(8)], 0).astype(np.float32)
    return (y_prompt, y_sample, new_state)
```
